# Optimizing a Trainium2 kernel written in Bass

```python
import jax, jax.numpy as jnp
from jax import lax
import numpy as np

D_MODEL = 1024
BATCH = 4
SEQ = 8192
DEPTH = 4

HEAD_DIM = 64
Q_BLOCK = 128
ROPE_THETA = 10000.0
EPS = 1e-6
NSA_HEADS = 8
NSA_KV_GROUPS = 2
NSA_CMP_LEN = 32
NSA_CMP_STRIDE = 16
NSA_CMP_HIDDEN = 128
NSA_SEL_LEN = 64
NSA_SEL_TOPK = 16
NSA_WINDOW = 512
FORCE_SCORE = 1e9
NSA_WIDTH = NSA_HEADS * HEAD_DIM
NSA_KV_WIDTH = NSA_KV_GROUPS * HEAD_DIM
FOX_HEADS = 8
FOX_WIDTH = FOX_HEADS * HEAD_DIM
MEM_LEN = 256
MEM_HEADS = 4
MEM_HEAD_DIM = 128
MEM_WIDTH = MEM_HEADS * MEM_HEAD_DIM
N_BRANCH = 3
IN_SPLITS = (NSA_WIDTH, 6 * NSA_KV_WIDTH, 3 * NSA_HEADS, NSA_WIDTH,
             3 * FOX_WIDTH, FOX_HEADS, FOX_WIDTH,
             MEM_WIDTH, MEM_WIDTH,
             N_BRANCH * D_MODEL)
IN_WIDTH = sum(IN_SPLITS)

kernel_name = 'hybrid_nsa_fox_memory_block'


def _rmsnorm(x, g):
    xf = x.astype(jnp.float32)
    y = xf * lax.rsqrt(jnp.mean(xf * xf, axis=-1, keepdims=True) + EPS)
    return (y * g.astype(jnp.float32)).astype(x.dtype)


def _rope(x, pos):
    half = x.shape[-1] // 2
    inv_freq = ROPE_THETA ** (-jnp.arange(half, dtype=jnp.float32) / half)
    ang = pos.astype(jnp.float32)[:, None] * inv_freq[None, :]
    cos = jnp.cos(ang)[None, :, None, :]
    sin = jnp.sin(ang)[None, :, None, :]
    xf = x.astype(jnp.float32)
    x1, x2 = xf[..., :half], xf[..., half:]
    return jnp.concatenate([x1 * cos - x2 * sin, x2 * cos + x1 * sin], axis=-1).astype(x.dtype)


def _masked_softmax(s, mask):
    s = jnp.where(mask, s.astype(jnp.float32), -jnp.inf)
    m = jnp.max(s, axis=-1, keepdims=True)
    m = jnp.where(jnp.isfinite(m), m, 0.0)
    e = jnp.exp(s - m)
    d = jnp.sum(e, axis=-1, keepdims=True)
    return e / jnp.where(d > 0.0, d, 1.0)


def _split(z, sizes):
    return jnp.split(z, np.cumsum(sizes)[:-1].tolist(), axis=-1)


def _compress(x, pe, w1, w2):
    B, T, G, dk = x.shape
    r = NSA_CMP_LEN // NSA_CMP_STRIDE
    n_chunk = T // NSA_CMP_STRIDE
    n_cmp = n_chunk - r + 1
    ch = x.reshape(B, n_chunk, NSA_CMP_STRIDE, G, dk)
    blk = jnp.concatenate([ch[:, i:i + n_cmp] for i in range(r)], axis=2)
    blk = blk + pe[None, None, :, None, :].astype(x.dtype)
    flat = blk.transpose(0, 1, 3, 2, 4).reshape(B, n_cmp, G, NSA_CMP_LEN * dk)
    return jax.nn.silu(flat @ w1) @ w2


def _overlap(n_cmp, n_slc):
    cs = np.arange(n_cmp)[:, None] * NSA_CMP_STRIDE
    ss = np.arange(n_slc)[None, :] * NSA_SEL_LEN
    ov = np.clip(np.minimum(cs + NSA_CMP_LEN, ss + NSA_SEL_LEN) - np.maximum(cs, ss), 0, None)
    return jnp.asarray(ov / NSA_CMP_LEN, dtype=jnp.float32)


def _nsa(q, kv, gate, pe_k, w1_k, w2_k, pe_v, w1_v, w2_v, q_g, k_g):
    B, T, _ = q.shape
    G, R, dk = NSA_KV_GROUPS, NSA_HEADS // NSA_KV_GROUPS, HEAD_DIM
    pos = jnp.arange(T)
    nqb = T // Q_BLOCK
    n_slc = T // NSA_SEL_LEN
    n_top = min(NSA_SEL_TOPK, n_slc)
    scale = dk ** -0.5
    qh = _rope(_rmsnorm(q.reshape(B, T, NSA_HEADS, dk), q_g), pos)
    k_c, v_c, k_s, v_s, k_w, v_w = [a.reshape(B, T, G, dk) for a in jnp.split(kv, 6, axis=-1)]
    k_c = _rmsnorm(_compress(_rope(k_c, pos), pe_k, w1_k, w2_k), k_g[0])
    v_c = _compress(v_c, pe_v, w1_v, w2_v)
    n_cmp = k_c.shape[1]
    k_s = _rope(_rmsnorm(k_s, k_g[1]), pos)
    k_w = _rope(_rmsnorm(k_w, k_g[2]), pos)
    ov = _overlap(n_cmp, n_slc)
    ks_blk = k_s.reshape(B, n_slc, NSA_SEL_LEN, G, dk).transpose(0, 3, 1, 2, 4)
    vs_blk = v_s.reshape(B, n_slc, NSA_SEL_LEN, G, dk).transpose(0, 3, 1, 2, 4)
    kw_pad = jnp.pad(k_w, ((0, 0), (NSA_WINDOW, 0), (0, 0), (0, 0)))
    vw_pad = jnp.pad(v_w, ((0, 0), (NSA_WINDOW, 0), (0, 0), (0, 0)))
    gather = jax.vmap(jax.vmap(lambda blocks, ix: blocks[ix]))
    cmp_end = jnp.arange(n_cmp) * NSA_CMP_STRIDE + NSA_CMP_LEN - 1
    jblk = jnp.arange(n_slc)
    qblocks = qh.reshape(B, nqb, Q_BLOCK, G, R, dk).transpose(1, 0, 2, 3, 4, 5)

    def block(args):
        qb, i = args
        t = i * Q_BLOCK + jnp.arange(Q_BLOCK)
        s = jnp.einsum('bqgrd,bngd->bgrqn', qb, k_c) * scale
        p_c = _masked_softmax(s, cmp_end[None, :] <= t[:, None])
        o_c = jnp.einsum('bgrqn,bngd->bqgrd', p_c.astype(v_c.dtype), v_c)
        imp = jnp.einsum('bgrqn,nj->bgqj', p_c, ov)
        cur = t // NSA_SEL_LEN
        forced = (jblk[None] == 0) | (jblk[None] == cur[:, None]) | (jblk[None] == cur[:, None] - 1)
        score = jnp.where(forced, FORCE_SCORE, imp)
        score = jnp.where(jblk[None] <= cur[:, None], score, -jnp.inf)
        _, idx = lax.top_k(score, n_top)
        sel_ok = idx <= cur[None, None, :, None]
        kg = gather(ks_blk, idx)
        vg = gather(vs_blk, idx)
        s = jnp.einsum('bqgrd,bgqnkd->bgrqnk', qb, kg) * scale
        kpos = idx[..., None] * NSA_SEL_LEN + jnp.arange(NSA_SEL_LEN)
        smask = sel_ok[..., None] & (kpos <= t[None, None, :, None, None])
        m_sel = n_top * NSA_SEL_LEN
        p_s = _masked_softmax(s.reshape(B, G, R, Q_BLOCK, m_sel),
                              smask[:, :, None].reshape(B, G, 1, Q_BLOCK, m_sel))
        o_s = jnp.einsum('bgrqm,bgqmd->bqgrd', p_s.astype(vg.dtype), vg.reshape(B, G, Q_BLOCK, m_sel, dk))
        kw = lax.dynamic_slice_in_dim(kw_pad, i * Q_BLOCK, Q_BLOCK + NSA_WINDOW, axis=1)
        vw = lax.dynamic_slice_in_dim(vw_pad, i * Q_BLOCK, Q_BLOCK + NSA_WINDOW, axis=1)
        spos = i * Q_BLOCK - NSA_WINDOW + jnp.arange(Q_BLOCK + NSA_WINDOW)
        diff = t[:, None] - spos[None, :]
        wmask = (diff >= 0) & (diff < NSA_WINDOW) & (spos[None, :] >= 0)
        s = jnp.einsum('bqgrd,bkgd->bgrqk', qb, kw) * scale
        p_w = _masked_softmax(s, wmask)
        o_w = jnp.einsum('bgrqk,bkgd->bqgrd', p_w.astype(vw.dtype), vw)
        return o_c, o_s, o_w

    o_c, o_s, o_w = lax.map(block, (qblocks, jnp.arange(nqb)))
    to_bthd = lambda o: o.transpose(1, 0, 2, 3, 4, 5).reshape(B, T, NSA_HEADS, dk)
    g = jax.nn.sigmoid(gate.astype(jnp.float32)).reshape(B, T, NSA_HEADS, 3, 1).astype(q.dtype)
    o = g[..., 0, :] * to_bthd(o_c) + g[..., 1, :] * to_bthd(o_s) + g[..., 2, :] * to_bthd(o_w)
    return o.reshape(B, T, NSA_WIDTH)


def _fox(qkv, f_logit, q_g, k_g, f_bias):
    B, T, _ = qkv.shape
    H, dk = FOX_HEADS, HEAD_DIM
    nqb = T // Q_BLOCK
    scale = dk ** -0.5
    q, k, v = [a.reshape(B, T, H, dk) for a in jnp.split(qkv, 3, axis=-1)]
    q = _rmsnorm(q, q_g)
    k = _rmsnorm(k, k_g)
    log_f = jax.nn.log_sigmoid(f_logit.astype(jnp.float32) + f_bias.astype(jnp.float32))
    c = jnp.cumsum(log_f, axis=1).transpose(0, 2, 1)
    qblocks = q.reshape(B, nqb, Q_BLOCK, H, dk).transpose(1, 0, 2, 3, 4)
    cblocks = c.reshape(B, H, nqb, Q_BLOCK).transpose(2, 0, 1, 3)
    kpos = jnp.arange(T)

    def block(args):
        qb, cq, i = args
        t = i * Q_BLOCK + jnp.arange(Q_BLOCK)
        s = jnp.einsum('bqhd,bkhd->bhqk', qb, k).astype(jnp.float32) * scale
        s = s + cq[..., None] - c[:, :, None, :]
        p = _masked_softmax(s, kpos[None, :] <= t[:, None])
        return jnp.einsum('bhqk,bkhd->bqhd', p.astype(v.dtype), v)

    o = lax.map(block, (qblocks, cblocks, jnp.arange(nqb)))
    return o.transpose(1, 0, 2, 3, 4).reshape(B, T, FOX_WIDTH)


def _memory_xattn(q, mem_kv, q_g, k_g):
    B, T, _ = q.shape
    M = mem_kv.shape[1]
    q = _rmsnorm(q.reshape(B, T, MEM_HEADS, MEM_HEAD_DIM), q_g)
    k, v = [a.reshape(B, M, MEM_HEADS, MEM_HEAD_DIM) for a in jnp.split(mem_kv, 2, axis=-1)]
    k = _rmsnorm(k, k_g)
    s = jnp.einsum('bthd,bmhd->bhtm', q, k).astype(jnp.float32) * (MEM_HEAD_DIM ** -0.5)
    p = jax.nn.softmax(s, axis=-1)
    return jnp.einsum('bhtm,bmhd->bthd', p.astype(v.dtype), v).reshape(B, T, MEM_WIDTH)


def setup_inputs(seed: int = 0) -> dict:
    key = jax.random.key(seed)
    ks = jax.random.split(key, 24)
    f32 = jnp.float32
    L = DEPTH

    def nrm(k, shape, scale):
        return jax.random.normal(k, shape, f32) * scale

    def gain(k, shape):
        return 1.0 + 0.05 * jax.random.normal(k, shape, f32)

    cmp_in = NSA_CMP_LEN * HEAD_DIM
    return {
        'x': nrm(ks[0], (BATCH, SEQ, D_MODEL), 1.0),
        'mem': nrm(ks[1], (BATCH, MEM_LEN, D_MODEL), 1.0),
        'norm_g': gain(ks[2], (L, D_MODEL)),
        'mem_norm_g': gain(ks[3], (L, D_MODEL)),
        'w_in': nrm(ks[4], (L, D_MODEL, IN_WIDTH), D_MODEL ** -0.5),
        'nsa_q_norm': gain(ks[5], (L, HEAD_DIM)),
        'nsa_k_norm': gain(ks[6], (L, 3, HEAD_DIM)),
        'cmp_pe_k': nrm(ks[7], (L, NSA_CMP_LEN, HEAD_DIM), 0.1),
        'cmp_w1_k': nrm(ks[8], (L, cmp_in, NSA_CMP_HIDDEN), cmp_in ** -0.5),
        'cmp_w2_k': nrm(ks[9], (L, NSA_CMP_HIDDEN, HEAD_DIM), NSA_CMP_HIDDEN ** -0.5),
        'cmp_pe_v': nrm(ks[10], (L, NSA_CMP_LEN, HEAD_DIM), 0.1),
        'cmp_w1_v': nrm(ks[11], (L, cmp_in, NSA_CMP_HIDDEN), cmp_in ** -0.5),
        'cmp_w2_v': nrm(ks[12], (L, NSA_CMP_HIDDEN, HEAD_DIM), NSA_CMP_HIDDEN ** -0.5),
        'fox_q_norm': gain(ks[13], (L, HEAD_DIM)),
        'fox_k_norm': gain(ks[14], (L, HEAD_DIM)),
        'fox_f_bias': jax.random.uniform(ks[15], (L, FOX_HEADS), f32, 1.0, 4.0),
        'mem_q_norm': gain(ks[16], (L, MEM_HEAD_DIM)),
        'mem_k_norm': gain(ks[17], (L, MEM_HEAD_DIM)),
        'w_mem_kv': nrm(ks[18], (L, D_MODEL, 2 * MEM_WIDTH), D_MODEL ** -0.5),
        'w_branch_a': nrm(ks[19], (L, NSA_WIDTH, D_MODEL), NSA_WIDTH ** -0.5),
        'w_branch_b': nrm(ks[20], (L, FOX_WIDTH, D_MODEL), FOX_WIDTH ** -0.5),
        'w_branch_m': nrm(ks[21], (L, MEM_WIDTH, D_MODEL), MEM_WIDTH ** -0.5),
        'w_out': nrm(ks[22], (L, D_MODEL, D_MODEL), D_MODEL ** -0.5),
    }


def reference(x, mem, norm_g, mem_norm_g, w_in, nsa_q_norm, nsa_k_norm,
              cmp_pe_k, cmp_w1_k, cmp_w2_k, cmp_pe_v, cmp_w1_v, cmp_w2_v,
              fox_q_norm, fox_k_norm, fox_f_bias, mem_q_norm, mem_k_norm, w_mem_kv,
              w_branch_a, w_branch_b, w_branch_m, w_out):
    B, T, D = x.shape
    for l in range(DEPTH):
        h = _rmsnorm(x, norm_g[l])
        z = h @ w_in[l]
        (nsa_q, nsa_kv, nsa_gate, nsa_silu, fox_qkv, fox_f, fox_silu,
         mem_q, mem_silu, merge_g) = _split(z, IN_SPLITS)
        o_a = _nsa(nsa_q, nsa_kv, nsa_gate, cmp_pe_k[l], cmp_w1_k[l], cmp_w2_k[l],
                   cmp_pe_v[l], cmp_w1_v[l], cmp_w2_v[l], nsa_q_norm[l], nsa_k_norm[l])
        o_a = o_a * jax.nn.silu(nsa_silu)
        o_b = _fox(fox_qkv, fox_f, fox_q_norm[l], fox_k_norm[l], fox_f_bias[l]) * jax.nn.silu(fox_silu)
        mem_kv = _rmsnorm(mem, mem_norm_g[l]) @ w_mem_kv[l]
        o_m = _memory_xattn(mem_q, mem_kv, mem_q_norm[l], mem_k_norm[l]) * jax.nn.silu(mem_silu)
        g = jax.nn.sigmoid(merge_g.astype(jnp.float32)).reshape(B, T, N_BRANCH, D).astype(x.dtype)
        u = (g[:, :, 0] * (o_a @ w_branch_a[l])
             + g[:, :, 1] * (o_b @ w_branch_b[l])
             + g[:, :, 2] * (o_m @ w_branch_m[l]))
        x = x + u @ w_out[l]
    return x
```

```python
import numpy as np
import concourse.bass as bass
import concourse.mybir as mybir

F32 = mybir.dt.float32
BF16 = mybir.dt.bfloat16
AF = mybir.ActivationFunctionType
ALU = mybir.AluOpType
AX = mybir.AxisListType


class Tok:
    __slots__ = ("w", "r")

    def __init__(self):
        self.w = None
        self.r = {}


class Buf:
    def __init__(self, t):
        self.t = t
        self.tok = Tok()

    def __getitem__(self, idx):
        return self.t[idx]


NDS = 8


class Prog:
    def __init__(self, nc):
        self.nc = nc
        self.eng = {"pe": nc.tensor, "act": nc.scalar, "dve": nc.vector, "pool": nc.gpsimd, "sp": nc.sync}
        self.ops = {k: [] for k in self.eng}
        self.cnt = {k: 0 for k in self.eng}
        self.sem = {k: nc.alloc_semaphore("sem_" + k) for k in self.eng}
        self.dsem = {k: [nc.alloc_semaphore("dq_%s_%d" % (k, i)) for i in range(NDS)] for k in ("sp", "pool", "act")}
        self.dcnt = {k: 0 for k in self.dsem}
        self.waited = {}
        self.semobj = {}
        self.n_ops = 0

    def _waits(self, eng, evs):
        best = {}
        for ev in evs:
            if ev is None:
                continue
            key, val = ev
            if self.waited.get((eng, key), 0) >= val:
                continue
            if best.get(key, 0) < val:
                best[key] = val
        out = []
        for key, val in best.items():
            self.waited[(eng, key)] = val
            out.append((self.semobj[key], val))
        return out

    def _deps(self, reads, writes):
        evs = []
        for b in reads:
            evs.append(b.tok.w)
        for b in writes:
            evs.append(b.tok.w)
            evs.extend(b.tok.r.values())
        return evs

    def _commit(self, ev, reads, writes, rkey):
        for b in reads:
            b.tok.r[rkey] = ev
        for b in writes:
            b.tok.w = ev
            b.tok.r = {}

    def op(self, eng, fn, reads=(), writes=()):
        evs = self._deps(reads, writes)
        self.cnt[eng] += 1
        key = "c_" + eng
        self.semobj[key] = self.sem[eng]
        ev = (key, self.cnt[eng])
        waits = self._waits(eng, evs)
        self.ops[eng].append((waits, fn, self.sem[eng], 1))
        self._commit(ev, reads, writes, key)
        self.n_ops += 1
        return ev

    def dma(self, q, out, in_, reads=(), writes=()):
        evs = self._deps(reads, writes)
        m = self.dcnt[q]
        self.dcnt[q] += 1
        slot = m % NDS
        key = "d_%s_%d" % (q, slot)
        self.semobj[key] = self.dsem[q][slot]
        prev = (key, 16 * (m // NDS)) if m >= NDS else None
        ev = (key, 16 * (m // NDS + 1))
        waits = self._waits(q, evs + [prev])
        self.ops[q].append((waits, lambda e, o=out, i=in_: e.dma_start(out=o, in_=i), self.dsem[q][slot], 16))
        self._commit(ev, reads, writes, "dq_%s_%d" % (q, m))
        self.n_ops += 1
        return ev

    def barrier(self, extra=()):
        evs = []
        for k in self.eng:
            if self.cnt[k]:
                self.semobj["c_" + k] = self.sem[k]
                evs.append(("c_" + k, self.cnt[k]))
        for q in self.dsem:
            m = self.dcnt[q]
            for s in range(NDS):
                n = (m - s + NDS - 1) // NDS if m > s else 0
                if n:
                    evs.append(("d_%s_%d" % (q, s), 16 * n))
        for k in self.eng:
            w = self._waits(k, evs)
            if w:
                self.ops[k].append((w, None, None, 0))

    def emit(self):
        nc = self.nc
        ops = self.ops
        with nc.Block() as block:
            def run(e, lst):
                for waits, fn, sem, inc in lst:
                    for s, v in waits:
                        e.wait_ge(s, v)
                    if fn is not None:
                        fn(e).then_inc(sem, inc)

            @block.tensor
            def _(e):
                run(e, ops["pe"])

            @block.scalar
            def _(e):
                run(e, ops["act"])

            @block.vector
            def _(e):
                run(e, ops["dve"])

            @block.gpsimd
            def _(e):
                run(e, ops["pool"])

            @block.sync
            def _(e):
                run(e, ops["sp"])
        self.ops = {k: [] for k in self.eng}

from contextlib import ExitStack
from concourse.bass_utils import run_bass_kernel_spmd
import ml_dtypes

EPS = 1e-6
BIG = 30000.0
CH = [(0, 512), (512, 512), (1024, 288), (1312, 512), (1824, 512), (2336, 512), (2848, 512),
      (3360, 512), (3872, 512), (4384, 512)]
WPIECES = [(0, 0, 1304), (1304, 3352, 8), (1312, 1304, 2048), (3360, 3360, 1536)]
R_QN, R_KC, R_VC, R_KS, R_KW, R_FQ, R_FK, R_MQ, R_SA, R_SB, R_SM = 0, 512, 640, 768, 896, 1024, 1536, 2048, 2560, 3072, 3584


def build(nc, T, L, dbg=False, phases="PC123E"):
    NT = T // 128
    NCMP = T // 16 - 1
    NCH = (NCMP + 127) // 128
    P = Prog(nc)
    din = lambda n, s, d=F32: nc.dram_tensor(n, s, d, kind="ExternalInput").ap()
    x_in = din("x", [T, 1024]); mem_in = din("mem", [256, 1024])
    norm_g = din("norm_g", [L, 1024]); mem_norm_g = din("mem_norm_g", [L, 1024])
    w_in = din("w_in", [L, 1024, 7968])
    nsa_q_norm = din("nsa_q_norm", [L, 64]); nsa_k_norm = din("nsa_k_norm", [L, 3, 64])
    cmp_pe = [din("cmp_pe_k", [L, 32, 64]), din("cmp_pe_v", [L, 32, 64])]
    cmp_w1 = [din("cmp_w1_k", [L, 2048, 128]), din("cmp_w1_v", [L, 2048, 128])]
    cmp_w2 = [din("cmp_w2_k", [L, 128, 64]), din("cmp_w2_v", [L, 128, 64])]
    fox_q_norm = din("fox_q_norm", [L, 64]); fox_k_norm = din("fox_k_norm", [L, 64]); fox_f_bias = din("fox_f_bias", [L, 8])
    mem_q_norm = din("mem_q_norm", [L, 128]); mem_k_norm = din("mem_k_norm", [L, 128])
    w_mem_kv = din("w_mem_kv", [L, 1024, 1024])
    w_br = [din("w_branch_a", [L, 512, 1024]), din("w_branch_b", [L, 512, 1024]), din("w_branch_m", [L, 512, 1024])]
    w_out = din("w_out", [L, 1024, 1024])
    c_ident = din("c_ident", [128, 128]); c_ropec = din("c_ropec", [T, 64]); c_ropes = din("c_ropes", [T, 64])
    c_tri = din("c_tri", [128, 128]); c_atri = din("c_atri", [128, 128]); c_sel = din("c_sel", [128, 128])
    c_eall = din("c_eall", [128, NT * 128]); c_cand = din("c_cand", [NT, 128, 128]); c_add = din("c_add", [NT, 128, 128])
    c_ov = din("c_ov", [NCH * 128, 128]); c_mask8 = din("c_mask8", [128, 8])
    out = nc.dram_tensor("out", [T, 1024], F32, kind="ExternalOutput").ap()
    sk = "ExternalOutput" if dbg else "Internal"
    FM = nc.dram_tensor("FM", [4096, T], BF16, kind=sk).ap()
    GT = nc.dram_tensor("GT", [32, T], F32, kind=sk).ap()
    TMV = nc.dram_tensor("TMV", [T, 12, 65], BF16, kind=sk).ap()
    OT = nc.dram_tensor("OT", [1536, T], BF16, kind=sk).ap()
    DBG = nc.dram_tensor("DBG", [3, 512, T], F32, kind=sk).ap() if dbg else None

    banks = [Buf(nc.alloc_psum_tensor("bank%d" % i, [128, 512], F32)) for i in range(8)]
    bctr = [0]

    def nb(lo=0, hi=8):
        b = banks[lo + bctr[0] % (hi - lo)]
        bctr[0] += 1
        return b

    gs = ExitStack()
    GA = lambda n, s, d: Buf(gs.enter_context(nc.sbuf_tensor(n, s, d)))
    identf = GA("identf", [128, 128], F32); identb = GA("identb", [128, 128], BF16)
    trib = GA("trib", [128, 128], BF16); atrib = GA("atrib", [128, 128], BF16)
    trif = GA("trif", [128, 128], F32); self_ = GA("self", [128, 128], F32); onesf = GA("onesf", [128, 128], F32)
    onesb = GA("onesb", [128, 128], BF16)
    mask8 = GA("mask8", [128, 8], F32)
    tmpf = GA("tmpf", [128, 128], F32)
    LF = GA("LF", [128, NT, 8], F32)
    KcT = [GA("KcT%d" % g, [64, NCH * 128], BF16) for g in range(2)]
    Vc = [GA("Vc%d" % g, [128, NCH, 64], BF16) for g in range(2)]

    def ACT(out_, in_, func, reads, writes, **kw):
        return P.op("act", lambda e: e.activation(out=out_, in_=in_, func=func, **kw), reads, writes)

    def TT(out_, a, b, op, reads, writes, eng="dve"):
        return P.op(eng, lambda e: e.tensor_tensor(out=out_, in0=a, in1=b, op=op), reads, writes)

    def TS(out_, a, s1, s2, op0, op1, reads, writes, eng="dve"):
        if op1 is None:
            return P.op(eng, lambda e: e.tensor_scalar(out=out_, in0=a, scalar1=s1, scalar2=None, op0=op0), reads, writes)
        return P.op(eng, lambda e: e.tensor_scalar(out=out_, in0=a, scalar1=s1, scalar2=s2, op0=op0, op1=op1), reads, writes)

    def CP(out_, in_, reads, writes, eng="dve"):
        return P.op(eng, lambda e: e.tensor_copy(out=out_, in_=in_), reads, writes)

    def MM(outb, specs, reads):
        def fn(e):
            for (o, l, r, st, sp) in specs:
                i = e.matmul(o, lhsT=l, rhs=r, start=st, stop=sp)
            return i
        return P.op("pe", fn, reads, [outb])

    def TR(outb, specs, reads):
        def fn(e):
            for (o, i_, idn) in specs:
                i = e.transpose(out=o, in_=i_, identity=idn)
            return i
        return P.op("pe", fn, reads, [outb])

    def RECIP(out_, in_, reads, writes):
        return P.op("dve", lambda e: e.reciprocal(out=out_, in_=in_), reads, writes)

    P.dma("sp", identf[:], c_ident[:, :], writes=[identf])
    CP(identb[:], identf[:], [identf], [identb], "pool")
    P.dma("sp", trif[:], c_tri[:, :], writes=[trif])
    CP(trib[:], trif[:], [trif], [trib], "pool")
    P.dma("sp", tmpf[:], c_atri[:, :], writes=[tmpf])
    CP(atrib[:], tmpf[:], [tmpf], [atrib], "pool")
    P.dma("sp", self_[:], c_sel[:, :], writes=[self_])
    P.dma("sp", mask8[:], c_mask8[:, :], writes=[mask8])
    P.op("pool", lambda e: e.memset(onesf[:], 1.0), [], [onesf])
    P.op("pool", lambda e: e.memset(onesb[:], 1.0), [], [onesb])

    def rmsnorm_rows(A, xt, gn, h, junk, ss):
        ACT(junk[:], xt[:], AF.Square, [xt], [junk, ss], accum_out=ss[:])
        TS(ss[:], ss[:], 1.0 / 1024, EPS, ALU.mult, ALU.add, [ss], [ss])
        ACT(ss[:], ss[:], AF.Sqrt, [ss], [ss])
        RECIP(ss[:], ss[:], [ss], [ss])
        P.op("dve", lambda e: e.scalar_tensor_tensor(out=h[:], in0=xt[:], scalar=ss[:, 0:1], in1=gn[:], op0=ALU.mult, op1=ALU.mult),
             [xt, ss, gn], [h])

    def transpose8(h, hT):
        b = nb()
        bv = b[:].bitcast(BF16).rearrange("p (c t) -> p c t", c=8)
        TR(b, [(bv[:, c, :], h[:, c * 128:(c + 1) * 128], identb[:]) for c in range(8)], [h, identb])
        ACT(hT[:], bv, AF.Copy, [b], [hT])

    def load_cast(stg, cnt, dst, dst_ap, src_ap, w, parts=128):
        s = stg[cnt[0] % len(stg)]
        cnt[0] += 1
        P.dma("sp", s[0:parts, 0:w], src_ap, writes=[s])
        CP(dst_ap, s[0:parts, 0:w], [s], [dst], "pool")

    class Pipe:
        def __init__(self, depth=2):
            self.q = []
            self.depth = depth

        def step(self, s_fn, mid_fn, pv_fn):
            s_fn()
            if len(self.q) >= self.depth:
                self.q.pop(0)()
            mid_fn()
            self.q.append(pv_fn)

        def flush(self):
            while self.q:
                self.q.pop(0)()

    def phase_P(l, x_src):
        with ExitStack() as es:
            A = lambda n, s, d: Buf(es.enter_context(nc.sbuf_tensor("%s_L%d" % (n, l), s, d)))
            Wp = A("Wp", [128, 8, 4896], BF16)
            stg = [A("stg%d" % i, [128, 2048], F32) for i in range(2)]
            cnt = [0]
            for kc in range(8):
                for (d0, s0, w) in WPIECES:
                    load_cast(stg, cnt, Wp, Wp[:, kc, d0:d0 + w], w_in[l, kc * 128:(kc + 1) * 128, s0:s0 + w], w)
            g1 = A("g1", [128, 64 * 6 + 128 + 8], F32)
            srcs = [nsa_q_norm[l], nsa_k_norm[l, 0], nsa_k_norm[l, 1], nsa_k_norm[l, 2], fox_q_norm[l], fox_k_norm[l]]
            for i, s in enumerate(srcs):
                P.dma("sp", g1[:, i * 64:(i + 1) * 64], s.partition_broadcast(128), writes=[g1])
            P.dma("sp", g1[:, 384:512], mem_q_norm[l].partition_broadcast(128), writes=[g1])
            P.dma("sp", g1[:, 512:520], fox_f_bias[l].partition_broadcast(128), writes=[g1])
            gn = A("gn", [128, 1024], F32)
            P.dma("sp", gn[:], norm_g[l].partition_broadcast(128), writes=[gn])
            Gq = A("Gq", [128, 512], F32); Gks = A("Gks", [128, 128], F32); Gkw = A("Gkw", [128, 128], F32)
            Gfq = A("Gfq", [128, 512], F32); Gfk = A("Gfk", [128, 512], F32); Gmq = A("Gmq", [128, 512], F32)

            def expand(G, H, D, off):
                CP(G[:].rearrange("p (h d) -> p h d", h=H), g1[:, off:off + D].unsqueeze(1).broadcast_to([128, H, D]), [g1], [G], "pool")
            expand(Gq, 8, 64, 0); expand(Gks, 2, 64, 128); expand(Gkw, 2, 64, 192)
            expand(Gfq, 8, 64, 256); expand(Gfk, 8, 64, 320); expand(Gmq, 4, 128, 384)
            xts = [A("xt%d" % i, [128, 1024], F32) for i in range(2)]
            css = [A("cs%d" % i, [128, 64], F32) for i in range(2)]
            sns = [A("sn%d" % i, [128, 64], F32) for i in range(2)]
            junk = A("junk", [128, 1024], BF16); ss = A("ss", [128, 1], F32)
            h = A("h", [128, 1024], BF16); hT = A("hT", [128, 8, 128], BF16)
            scr = [dict(sq=A("sq%d" % i, [128, 512], F32), xs=A("xs%d" % i, [128, 512], F32), ssq=A("ssq%d" % i, [128, 8], F32),
                        ta=A("ta%d" % i, [128, 512], F32), tb=A("tb%d" % i, [128, 512], F32)) for i in range(2)]
            sctr = [0]
            zb = [A("zb%d" % i, [128, 512], BF16) for i in range(8)]
            stage = A("stage", [128, 32, 128], BF16)
            vts = [A("vt%d" % i, [128, 12, 65], BF16) for i in range(2)]
            for v in vts:
                P.op("pool", lambda e, v=v: e.memset(v[:], 1.0), [], [v])
            gtt = A("gtt", [128, 32], F32); gts = A("gts", [32, 128], F32)
            ft = A("ft", [128, 8], F32); sgm = A("sgm", [128, 512], F32)

            def norm_rope(psb, ps_ap, W, H, D, G, rope, dst, dst_ap, cs, sn):
                S = scr[sctr[0] % 2]
                sctr[0] += 1
                sq, xs, ssq, ta, tb = S["sq"], S["xs"], S["ssq"], S["ta"], S["tb"]
                v3 = lambda ap: ap.rearrange("p (h d) -> p h d", h=H)
                ACT(xs[:, :W], ps_ap, AF.Copy, [psb], [xs])
                if G is not None:
                    ACT(sq[:, :W], ps_ap, AF.Square, [psb], [sq])
                    P.op("dve", lambda e: e.tensor_reduce(out=ssq[:, :H], in_=v3(sq[:, :W]), axis=AX.X, op=ALU.add), [sq], [ssq])
                    TS(ssq[:, :H], ssq[:, :H], 1.0 / D, EPS, ALU.mult, ALU.add, [ssq], [ssq])
                    ACT(ssq[:, :H], ssq[:, :H], AF.Sqrt, [ssq], [ssq])
                    RECIP(ssq[:, :H], ssq[:, :H], [ssq], [ssq])
                    TT(v3(xs[:, :W]), v3(xs[:, :W]), ssq[:, :H].unsqueeze(2).broadcast_to([128, H, D]), ALU.mult, [xs, ssq], [xs])
                    if not rope:
                        TT(dst_ap, xs[:, :W], G[:, :W], ALU.mult, [xs, G], [dst])
                        return
                    TT(xs[:, :W], xs[:, :W], G[:, :W], ALU.mult, [xs, G], [xs])
                x4 = xs[:, :W].rearrange("p (h two e) -> p h two e", h=H, two=2)
                t4 = tb[:, :W].rearrange("p (h two e) -> p h two e", h=H, two=2)
                TT(v3(ta[:, :W]), v3(xs[:, :W]), cs[:].unsqueeze(1).broadcast_to([128, H, 64]), ALU.mult, [xs, cs], [ta])
                TT(t4[:, :, 0, :], x4[:, :, 1, :], sn[:, 0:32].unsqueeze(1).broadcast_to([128, H, 32]), ALU.mult, [xs, sn], [tb])
                TT(t4[:, :, 1, :], x4[:, :, 0, :], sn[:, 32:64].unsqueeze(1).broadcast_to([128, H, 32]), ALU.mult, [xs, sn], [tb])
                TT(dst_ap, ta[:, :W], tb[:, :W], ALU.add, [ta, tb], [dst])

            def silu_to(psb, ps_ap, dst, dst_ap):
                ACT(sgm[:], ps_ap, AF.Sigmoid, [psb], [sgm])
                TT(dst_ap, ps_ap, sgm[:], ALU.mult, [psb, sgm], [dst])

            for tt in range(NT):
                xt = xts[tt % 2]; cs = css[tt % 2]; sn = sns[tt % 2]; vt = vts[tt % 2]
                tsl = slice(tt * 128, (tt + 1) * 128)
                P.dma("sp", xt[:], x_src[tsl, :], writes=[xt])
                P.dma("sp", cs[:], c_ropec[tsl, :], writes=[cs])
                P.dma("sp", sn[:], c_ropes[tsl, :], writes=[sn])
                rmsnorm_rows(A, xt, gn, h, junk, ss)
                transpose8(h, hT)
                for cc in range(10):
                    off, w = CH[cc]
                    b = nb()
                    MM(b, [(b[:, 0:w], hT[:, kc, :], Wp[:, kc, off:off + w], kc == 0, kc == 7) for kc in range(8)], [hT, Wp])
                    if cc == 0:
                        norm_rope(b, b[:, 0:512], 512, 8, 64, Gq, True, zb[0], zb[0][:, :], cs, sn)
                    elif cc == 1:
                        norm_rope(b, b[:, 0:128], 128, 2, 64, None, True, zb[1], zb[1][:, 0:128], cs, sn)
                        ACT(zb[1][:, 128:256], b[:, 128:256], AF.Copy, [b], [zb[1]])
                        norm_rope(b, b[:, 256:384], 128, 2, 64, Gks, True, zb[1], zb[1][:, 256:384], cs, sn)
                        ACT(vt[:, 0:2, 0:64], b[:, 384:512].rearrange("p (h d) -> p h d", h=2), AF.Copy, [b], [vt])
                    elif cc == 2:
                        norm_rope(b, b[:, 0:128], 128, 2, 64, Gkw, True, zb[1], zb[1][:, 384:512], cs, sn)
                        ACT(vt[:, 2:4, 0:64], b[:, 128:256].rearrange("p (h d) -> p h d", h=2), AF.Copy, [b], [vt])
                        ACT(gtt[:, 0:24], b[:, 256:280], AF.Sigmoid, [b], [gtt])
                        TT(ft[:], b[:, 280:288], g1[:, 512:520], ALU.add, [b, g1], [ft])
                        ACT(ft[:], ft[:], AF.Sigmoid, [ft], [ft])
                        ACT(LF[:, tt, :], ft[:], AF.Ln, [ft], [LF])
                    elif cc == 3:
                        silu_to(b, b[:, 0:512], zb[5], zb[5][:, :])
                    elif cc == 4:
                        norm_rope(b, b[:, 0:512], 512, 8, 64, Gfq, False, zb[2], zb[2][:, :], cs, sn)
                    elif cc == 5:
                        norm_rope(b, b[:, 0:512], 512, 8, 64, Gfk, False, zb[3], zb[3][:, :], cs, sn)
                    elif cc == 6:
                        ACT(vt[:, 4:12, 0:64], b[:, 0:512].rearrange("p (h d) -> p h d", h=8), AF.Copy, [b], [vt])
                    elif cc == 7:
                        silu_to(b, b[:, 0:512], zb[6], zb[6][:, :])
                    elif cc == 8:
                        norm_rope(b, b[:, 0:512], 512, 4, 128, Gmq, False, zb[4], zb[4][:, :], cs, sn)
                    elif cc == 9:
                        silu_to(b, b[:, 0:512], zb[7], zb[7][:, :])
                for pr in range(4):
                    b = nb()
                    bv = b[:].bitcast(BF16).rearrange("p (c t) -> p c t", c=8)
                    specs = []
                    for j in range(8):
                        z = zb[pr * 2 + j // 4]
                        specs.append((bv[:, j, :], z[:, (j % 4) * 128:(j % 4 + 1) * 128], identb[:]))
                    TR(b, specs, [zb[pr * 2], zb[pr * 2 + 1], identb])
                    if pr % 2 == 0:
                        ACT(stage[:, pr * 8:(pr + 1) * 8, :], bv, AF.Copy, [b], [stage])
                    else:
                        CP(stage[:, pr * 8:(pr + 1) * 8, :], bv, [b], [stage])
                P.dma("pool", FM[:, tsl].rearrange("(b p) t -> p b t", p=128), stage[:], reads=[stage])
                P.dma("pool", TMV[tsl, :, :], vt[:], reads=[vt])
                b = nb()
                TR(b, [(b[0:24, 0:128], gtt[:, 0:24], identf[:])], [gtt, identf])
                CP(gts[0:24, :], b[0:24, 0:128], [b], [gts])
                P.dma("pool", GT[0:24, tsl], gts[0:24, :], reads=[gts])
            P.barrier()
            P.emit()

    def phase_C(l):
        with ExitStack() as es:
            A = lambda n, s, d: Buf(es.enter_context(nc.sbuf_tensor("%s_L%d" % (n, l), s, d)))
            XTs = [A("XT%d" % i, [64, T], BF16) for i in range(2)]
            stg = [A("cstg%d" % i, [128, 2048], F32) for i in range(2)]
            cnt = [0]
            w1b = A("w1b", [64, 32, 128], BF16); w2b = A("w2b", [128, 64], BF16)
            pef = A("pef", [32, 64], F32); peT = A("peT", [64, 32], BF16); cst = A("cst", [128, 1], F32)
            hidT = A("hidT", [128, 512], BF16); sgc = A("sgc", [128, 512], F32)
            g0 = A("g0", [128, 64], F32); kn = A("kn", [128, 64], BF16)
            ssq = A("cssq", [128, 1], F32); junk = A("cjunk", [128, 64], F32)
            P.dma("sp", g0[:], nsa_k_norm[l, 0].partition_broadcast(128), writes=[g0])
            xi = 0
            for kv in range(2):
                w1v = cmp_w1[kv][l].rearrange("(l d) h -> d l h", d=64)
                for hf in range(2):
                    s = stg[cnt[0] % 2]; cnt[0] += 1
                    P.dma("sp", s[0:64, :].rearrange("p (l h) -> p l h", l=16), w1v[:, hf * 16:(hf + 1) * 16, :], writes=[s])
                    CP(w1b[:, hf * 16:(hf + 1) * 16, :], s[0:64, :].rearrange("p (l h) -> p l h", l=16), [s], [w1b], "pool")
                load_cast(stg, cnt, w2b, w2b[:], cmp_w2[kv][l], 64)
                P.dma("sp", pef[:], cmp_pe[kv][l], writes=[pef])
                b = nb()
                TR(b, [(b[0:64, 0:32], pef[:], identf[0:32, 0:32])], [pef, identf])
                CP(peT[:], b[0:64, 0:32], [b], [peT])
                b = nb()
                MM(b, [(b[:, 0:1], w1b[:, l_, :], peT[:, l_:l_ + 1], l_ == 0, l_ == 31) for l_ in range(32)], [w1b, peT])
                CP(cst[:], b[:, 0:1], [b], [cst])
                for g in range(2):
                    XT = XTs[xi % 2]; xi += 1
                    row = (R_KC if kv == 0 else R_VC) + g * 64
                    P.dma("sp", XT[:], FM[row:row + 64, :], writes=[XT])
                    X3 = XT[:, :].rearrange("p (n s) -> p n s", s=16)
                    b = nb()
                    MM(b, [(b[:, 0:NCMP], w1b[:, l_, :], X3[:, l_ // 16:l_ // 16 + NCMP, l_ % 16], l_ == 0, l_ == 31) for l_ in range(32)],
                       [w1b, XT])
                    ACT(sgc[:, 0:NCMP], b[:, 0:NCMP], AF.Sigmoid, [b, cst], [sgc], bias=cst[:, 0:1])
                    P.op("dve", lambda e, b=b: e.scalar_tensor_tensor(out=hidT[:, 0:NCMP], in0=b[:, 0:NCMP], scalar=cst[:, 0:1], in1=sgc[:, 0:NCMP],
                                                                    op0=ALU.add, op1=ALU.mult), [b, cst, sgc], [hidT])
                    for c in range(NCH):
                        rows = min(128, NCMP - c * 128)
                        b2 = nb()
                        MM(b2, [(b2[0:rows, 0:64], hidT[:, c * 128:c * 128 + rows], w2b[:], True, True)], [hidT, w2b])
                        if kv == 0:
                            ACT(junk[0:rows, :], b2[0:rows, 0:64], AF.Square, [b2], [junk, ssq], accum_out=ssq[0:rows, :])
                            TS(ssq[0:rows, :], ssq[0:rows, :], 1.0 / 64, EPS, ALU.mult, ALU.add, [ssq], [ssq])
                            ACT(ssq[0:rows, :], ssq[0:rows, :], AF.Sqrt, [ssq], [ssq])
                            RECIP(ssq[0:rows, :], ssq[0:rows, :], [ssq], [ssq])
                            P.op("dve", lambda e, b2=b2, rows=rows: e.scalar_tensor_tensor(out=kn[0:rows, :], in0=b2[0:rows, 0:64], scalar=ssq[0:rows, 0:1],
                                                                                          in1=g0[0:rows, :], op0=ALU.mult, op1=ALU.mult), [b2, ssq, g0], [kn])
                            b3 = nb()
                            bv = b3[:].bitcast(BF16)
                            TR(b3, [(bv[0:64, 0:rows], kn[0:rows, :], identb[0:rows, 0:rows])], [kn, identb])
                            CP(KcT[g][:, c * 128:c * 128 + rows], bv[0:64, 0:rows], [b3], [KcT[g]])
                        else:
                            CP(Vc[g][0:rows, c, :], b2[0:rows, 0:64], [b2], [Vc[g]])
            P.barrier()
            P.emit()

    def phase_A1(l):
        with ExitStack() as es:
            A = lambda n, s, d: Buf(es.enter_context(nc.sbuf_tensor("%s_L%d" % (n, l), s, d)))
            stg = [A("astg%d" % i, [128, 1024], F32) for i in range(2)]
            cnt = [0]
            Eall = A("Eall", [128, NT, 128], BF16)
            Ef = Eall[:].rearrange("p c k -> p (c k)")
            for o in range(0, NT * 128, 1024):
                load_cast(stg, cnt, Eall, Ef[:, o:o + 1024], c_eall[:, o:o + 1024], 1024)
            ovb = A("ovb", [128, NCH, 128], BF16)
            for c in range(NCH):
                load_cast(stg, cnt, ovb, ovb[:, c, :], c_ov[c * 128:(c + 1) * 128, :], 128)
            QT4 = A("QT4", [64, 4, T], BF16); KsT = A("KsT", [64, T], BF16); KwT = A("KwT", [64, T], BF16)
            Vs = A("Vs", [128, NT, 65], BF16); Vw = A("Vw", [128, NT, 65], BF16)
            cands = [A("cand%d" % i, [128, 128], F32) for i in range(2)]
            adds = [A("addm%d" % i, [128, 128], F32) for i in range(2)]
            Es = [A("E%d" % i, [128, 512], F32) for i in range(2)]
            Pns = [A("Pn%d" % i, [128, 512], BF16) for i in range(2)]
            PnT = [A("PnT%d" % i, [128, 4, 128], BF16) for i in range(4)]
            rss = [A("rs%d" % i, [128, 1], F32) for i in range(2)]
            sc = A("sc", [128, 128], F32); scr = A("scr", [128, 128], F32)
            m8a = A("m8a", [128, 8], F32); m8b = A("m8b", [128, 8], F32); thr = A("thr", [128, 1], F32)
            M1 = A("M1", [128, 128], BF16); MT4 = A("MT4", [128, 4, 128], BF16)
            PTs = [A("PT%d" % i, [128, 512], BF16) for i in range(5)]
            pk = [0]
            gl = A("gl", [65, 3, 512], F32); Fr = A("Fr", [65, 3, 512], F32); Rsb = A("Rsb", [64, 3, 512], F32)
            t1 = A("t1", [64, 512], F32); t2 = A("t2", [64, 512], F32)
            siTs = [A("siT%d" % i, [64, 4, 128], BF16) for i in range(2)]
            oaTs = [A("oaT%d" % i, [64, 4, 128], BF16) for i in range(2)]
            Bos, Bow, Boc, Bimp = banks[3], banks[4], banks[5], banks[7]
            sct = [0]

            def nbS():
                b = banks[(0, 1, 2, 6)[sct[0] % 4]]
                sct[0] += 1
                return b
            tri4 = trib[:].unsqueeze(1).broadcast_to([128, 4, 128])
            atri4 = atrib[:].unsqueeze(1).broadcast_to([128, 4, 128])
            v4 = lambda ap: ap.rearrange("p (r q) -> p r q", r=4)
            for g in range(2):
                P.dma("sp", QT4[:], FM[R_QN + g * 256:R_QN + (g + 1) * 256, :].rearrange("(r d) t -> d r t", d=64), writes=[QT4])
                P.dma("sp", KsT[:], FM[R_KS + g * 64:R_KS + (g + 1) * 64, :], writes=[KsT])
                P.dma("sp", KwT[:], FM[R_KW + g * 64:R_KW + (g + 1) * 64, :], writes=[KwT])
                P.dma("sp", Vs[:], TMV[:, g, :].rearrange("(c p) e -> p c e", p=128), writes=[Vs])
                P.dma("sp", Vw[:], TMV[:, 2 + g, :].rearrange("(c p) e -> p c e", p=128), writes=[Vw])
                for i in range(NT):
                    isl = slice(i * 128, (i + 1) * 128)
                    cand = cands[i % 2]; addm = adds[i % 2]; siT = siTs[i % 2]; oaT = oaTs[i % 2]
                    P.dma("sp", cand[:], c_cand[i], writes=[cand])
                    P.dma("sp", addm[:], c_add[i], writes=[addm])
                    P.dma("sp", siT[:], FM[R_SA + g * 256:R_SA + (g + 1) * 256, isl].rearrange("(r d) q -> d r q", d=64), writes=[siT])
                    P.dma("sp", gl[64:65, :, :].rearrange("p x (r q) -> p x r q", r=4),
                          GT[12 * g:12 * g + 12, isl].rearrange("(r x) q -> x r q", x=3).unsqueeze(0), writes=[gl])
                    ncols = min(8 * i + 7, NCMP)
                    nchk = (ncols + 127) // 128
                    m0 = 1 if i == 0 else 0
                    lo = 8 * i - 1 + m0
                    for r in range(4):
                        b = nbS()
                        MM(b, [(b[:, 0:ncols], QT4[:, r, isl], KcT[g][:, 0:ncols], True, True)], [QT4, KcT[g]])
                        E = Es[r % 2]; Pn = Pns[r % 2]; rs = rss[r % 2]
                        ACT(E[:, 0:ncols], b[:, 0:ncols], AF.Exp, [b], [E], scale=0.125)
                        TT(E[:, lo:ncols], E[:, lo:ncols], mask8[:, m0:m0 + ncols - lo], ALU.mult, [E, mask8], [E])
                        P.op("dve", lambda e, E=E, rs=rs, nco=ncols: e.tensor_reduce(out=rs[:], in_=E[:, 0:nco], axis=AX.X, op=ALU.add), [E], [rs])
                        TS(rs[:], rs[:], 1e-30, None, ALU.max, None, [rs], [rs])
                        RECIP(rs[:], rs[:], [rs], [rs])
                        TS(Pn[:, 0:ncols], E[:, 0:ncols], rs[:, 0:1], None, ALU.mult, None, [E, rs], [Pn])
                        ws = [min(128, ncols - c * 128) for c in range(nchk)]
                        Btr = nbS()
                        btv = Btr[:].bitcast(BF16)
                        TR(Btr, [(btv[0:ws[c], c * 128:(c + 1) * 128], Pn[:, c * 128:c * 128 + ws[c]], identb[:]) for c in range(nchk)], [Pn, identb])
                        for c in range(nchk):
                            if c % 2 == 0:
                                ACT(PnT[r][0:ws[c], c, :], btv[0:ws[c], c * 128:(c + 1) * 128], AF.Copy, [Btr], [PnT[r]])
                            else:
                                CP(PnT[r][0:ws[c], c, :], btv[0:ws[c], c * 128:(c + 1) * 128], [Btr], [PnT[r]])
                        MM(Boc, [(Boc[0:64, r * 128:(r + 1) * 128], Vc[g][0:ws[c], c, :], PnT[r][0:ws[c], c, :], c == 0, c == nchk - 1)
                                 for c in range(nchk)], [Vc[g], PnT[r]])
                        MM(Bimp, [(Bimp[:, 0:128], PnT[r][0:ws[c], c, :], ovb[0:ws[c], c, :], r == 0 and c == 0, r == 3 and c == nchk - 1)
                                  for c in range(nchk)], [PnT[r], ovb])
                    TT(sc[:], Bimp[:, 0:128], cand[:], ALU.mult, [Bimp, cand], [sc])
                    TT(sc[:], sc[:], addm[:], ALU.add, [sc, addm], [sc])
                    P.op("dve", lambda e: e.max(out=m8a[:], in_=sc[:]), [sc], [m8a])
                    P.op("dve", lambda e: e.match_replace(out=scr[:], in_to_replace=m8a[:], in_values=sc[:], imm_value=-2.0), [sc, m8a], [scr])
                    P.op("dve", lambda e: e.max(out=m8b[:], in_=scr[:]), [scr], [m8b])
                    TS(thr[:], m8b[:, 7:8], 0.0, None, ALU.max, None, [m8b], [thr])
                    TS(M1[:], sc[:], thr[:, 0:1], 1.0, ALU.is_ge, ALU.subtract, [sc, thr], [M1])
                    Btr = nbS()
                    btv = Btr[:].bitcast(BF16)
                    TR(Btr, [(btv[:, 0:128], M1[:], identb[:])], [M1, identb])
                    ACT(MT4[:], btv[:, 0:128].unsqueeze(1).broadcast_to([128, 4, 128]), AF.Copy, [Btr], [MT4], scale=BIG)
                    pipe = Pipe()
                    c0 = max(0, i - 4)

                    def sel_step(c):
                        csl = slice(c * 128, (c + 1) * 128)
                        b = nbS()
                        pt = PTs[pk[0] % len(PTs)]; pk[0] += 1

                        def s_fn():
                            MM(b, [(v4(b[:, :]), KsT[:, csl], QT4[:, :, isl], True, False),
                                   (v4(b[:, :]), Eall[:, c, :], MT4[:], False, True)], [KsT, QT4, Eall, MT4])

                        def mid_fn():
                            ACT(pt[:], b[:, :], AF.Exp, [b], [pt], scale=0.125)
                            if c == i:
                                TT(v4(pt[:]), v4(pt[:]), tri4, ALU.mult, [pt, trib], [pt])

                        def pv_fn():
                            MM(Bos, [(Bos[0:65, :], Vs[:, c, :], pt[:], c == 0, c == i)], [Vs, pt])
                        pipe.step(s_fn, mid_fn, pv_fn)

                    def win_step(c):
                        csl = slice(c * 128, (c + 1) * 128)
                        b = nbS()
                        pt = PTs[pk[0] % len(PTs)]; pk[0] += 1

                        def s_fn():
                            MM(b, [(v4(b[:, :]), KwT[:, csl], QT4[:, :, isl], True, True)], [KwT, QT4])

                        def mid_fn():
                            ACT(pt[:], b[:, :], AF.Exp, [b], [pt], scale=0.125)
                            if c == i:
                                TT(v4(pt[:]), v4(pt[:]), tri4, ALU.mult, [pt, trib], [pt])
                            if c == i - 4:
                                TT(v4(pt[:]), v4(pt[:]), atri4, ALU.mult, [pt, atrib], [pt])

                        def pv_fn():
                            MM(Bow, [(Bow[0:65, :], Vw[:, c, :], pt[:], c == c0, c == i)], [Vw, pt])
                        pipe.step(s_fn, mid_fn, pv_fn)
                    for c in range(c0, i + 1):
                        win_step(c)
                    for c in range(i + 1):
                        sel_step(c)
                    pipe.flush()
                    CP(Fr[64:65, 0, :], gl[64:65, 0, :], [gl], [Fr])
                    RECIP(Fr[64:65, 1, :], Bos[64:65, :], [Bos], [Fr])
                    TT(Fr[64:65, 1, :], Fr[64:65, 1, :], gl[64:65, 1, :], ALU.mult, [Fr, gl], [Fr])
                    RECIP(Fr[64:65, 2, :], Bow[64:65, :], [Bow], [Fr])
                    TT(Fr[64:65, 2, :], Fr[64:65, 2, :], gl[64:65, 2, :], ALU.mult, [Fr, gl], [Fr])
                    for x_ in range(3):
                        b = nbS()
                        MM(b, [(b[0:64, :], onesf[64:65, 0:64], Fr[64:65, x_, :], True, True)], [onesf, Fr])
                        ACT(Rsb[:, x_, :], b[0:64, :], AF.Copy, [b], [Rsb])
                    def dump(x_, tb_):
                        if dbg:
                            P.dma("pool", DBG[x_, g * 256:(g + 1) * 256, isl].rearrange("(r d) q -> d r q", d=64), v4(tb_[:]), reads=[tb_])
                    TT(t1[:], Boc[0:64, :], Rsb[:, 0, :], ALU.mult, [Boc, Rsb], [t1])
                    dump(0, t1)
                    TT(t2[:], Bos[0:64, :], Rsb[:, 1, :], ALU.mult, [Bos, Rsb], [t2])
                    dump(1, t2)
                    TT(t1[:], t1[:], t2[:], ALU.add, [t1, t2], [t1])
                    TT(t2[:], Bow[0:64, :], Rsb[:, 2, :], ALU.mult, [Bow, Rsb], [t2])
                    dump(2, t2)
                    TT(t1[:], t1[:], t2[:], ALU.add, [t1, t2], [t1])
                    TT(oaT[:], v4(t1[:]), siT[:], ALU.mult, [t1, siT], [oaT])
                    P.dma("pool", OT[g * 256:(g + 1) * 256, isl].rearrange("(r d) q -> d r q", d=64), oaT[:], reads=[oaT])
            P.barrier()
            P.emit()

    def phase_A2(l):
        with ExitStack() as es:
            A = lambda n, s, d: Buf(es.enter_context(nc.sbuf_tensor("%s_L%d" % (n, l), s, d)))
            Xa = A("Xa", [128, NT, 8], F32); Xb = A("Xb", [128, NT, 8], F32)
            C = A("C", [128, NT, 8], F32); CR = A("CR", [128, NT, 8], F32)
            BAs = [A("BA%d" % i, [128, NT, NT], F32) for i in range(2)]
            FqTs = [A("FqT%d" % i, [64, T], BF16) for i in range(2)]
            FkTs = [A("FkT%d" % i, [64, T], BF16) for i in range(2)]
            Fvs = [A("Fv%d" % i, [128, NT, 65], BF16) for i in range(2)]
            PTs = [A("fPT%d" % i, [128, 512], BF16) for i in range(5)]
            rinv = A("rinv", [65, 512], F32); Rsb = A("fRsb", [64, 512], F32); t1 = A("ft1", [64, 512], F32)
            siTs = [A("fsiT%d" % i, [64, 512], BF16) for i in range(2)]
            obTs = [A("obT%d" % i, [64, 512], BF16) for i in range(2)]
            CP(Xa[:], LF[:], [LF], [Xa])
            src, dst = Xa, Xb
            s = 1
            while s < NT:
                TT(dst[:, s:, :], src[:, s:, :], src[:, :NT - s, :], ALU.add, [src], [dst])
                CP(dst[:, :s, :], src[:, :s, :], [src], [dst])
                src, dst = dst, src
                s *= 2
            TT(dst[:], src[:], LF[:], ALU.subtract, [src, LF], [dst])
            Xex = dst
            b = banks[7]
            f2 = lambda ap: ap.rearrange("p c h -> p (c h)")
            MM(b, [(b[:, 0:NT * 8], trif[:], f2(LF[:]), True, False), (b[:, 0:NT * 8], onesf[:], f2(Xex[:]), False, True)], [trif, LF, onesf, Xex])
            CP(f2(C[:]), b[:, 0:NT * 8], [b], [C])
            b = banks[6]
            MM(b, [(b[:, 0:NT * 8], self_[:], f2(C[:]), True, True)], [self_, C])
            CP(f2(CR[:]), b[:, 0:NT * 8], [b], [CR])
            sct = [0]; pk = [0]

            def nbS():
                bb = banks[sct[0] % 4]
                sct[0] += 1
                return bb
            qi = 0
            for hh in range(8):
                BA = BAs[hh % 2]; FqT = FqTs[hh % 2]; FkT = FkTs[hh % 2]; Fv = Fvs[hh % 2]
                TT(BA[:], CR[:, :, hh].unsqueeze(2).broadcast_to([128, NT, NT]), C[:, :, hh].unsqueeze(1).broadcast_to([128, NT, NT]),
                   ALU.subtract, [CR, C], [BA])
                P.dma("sp", FqT[:], FM[R_FQ + hh * 64:R_FQ + (hh + 1) * 64, :], writes=[FqT])
                P.dma("sp", FkT[:], FM[R_FK + hh * 64:R_FK + (hh + 1) * 64, :], writes=[FkT])
                P.dma("sp", Fv[:], TMV[:, 4 + hh, :].rearrange("(c p) e -> p c e", p=128), writes=[Fv])
                for qg in range(NT // 4):
                    qsl = slice(qg * 512, (qg + 1) * 512)
                    Bo = banks[4 + qi % 2]; siT = siTs[qi % 2]; obT = obTs[qi % 2]; qi += 1
                    P.dma("sp", siT[:], FM[R_SB + hh * 64:R_SB + (hh + 1) * 64, qsl], writes=[siT])
                    nk = 4 * qg + 4
                    pipe = Pipe()

                    def fox_step(kc):
                        b = nbS()
                        pt = PTs[pk[0] % len(PTs)]; pk[0] += 1
                        j0 = max(0, kc - 4 * qg)

                        def s_fn():
                            MM(b, [(b[:, :], FkT[:, kc * 128:(kc + 1) * 128], FqT[:, qsl], True, True)], [FkT, FqT])

                        def mid_fn():
                            for s_ in range(j0, 4):
                                qb = 4 * qg + s_
                                ssl = slice(s_ * 128, (s_ + 1) * 128)
                                ACT(pt[:, ssl], b[:, ssl], AF.Exp, [b, BA], [pt], scale=0.125, bias=BA[:, qb, kc:kc + 1])
                            if kc >= 4 * qg:
                                ssl = slice(j0 * 128, (j0 + 1) * 128)
                                TT(pt[:, ssl], pt[:, ssl], trib[:], ALU.mult, [pt, trib], [pt])

                        def pv_fn():
                            MM(Bo, [(Bo[0:65, j0 * 128:512], Fv[:, kc, :], pt[:, j0 * 128:512], kc == 0, kc == nk - 1)], [Fv, pt])
                        pipe.step(s_fn, mid_fn, pv_fn)
                    for kc in range(nk):
                        fox_step(kc)
                    pipe.flush()
                    RECIP(rinv[64:65, :], Bo[64:65, :], [Bo], [rinv])
                    b = banks[6 + qi % 2]
                    MM(b, [(b[0:64, :], onesf[64:65, 0:64], rinv[64:65, :], True, True)], [onesf, rinv])
                    ACT(Rsb[:], b[0:64, :], AF.Copy, [b], [Rsb])
                    TT(t1[:], Bo[0:64, :], Rsb[:], ALU.mult, [Bo, Rsb], [t1])
                    TT(obT[:], t1[:], siT[:], ALU.mult, [t1, siT], [obT])
                    P.dma("pool", OT[512 + hh * 64:512 + (hh + 1) * 64, qsl], obT[:], reads=[obT])
            P.barrier()
            P.emit()

    def phase_A3(l):
        with ExitStack() as es:
            A = lambda n, s, d: Buf(es.enter_context(nc.sbuf_tensor("%s_L%d" % (n, l), s, d)))
            stg = [A("mstg%d" % i, [128, 1024], F32) for i in range(2)]
            cnt = [0]
            Wm = A("Wm", [128, 8, 1024], BF16)
            for kc in range(8):
                load_cast(stg, cnt, Wm, Wm[:, kc, :], w_mem_kv[l, kc * 128:(kc + 1) * 128, :], 1024)
            gm = A("gm", [128, 1024], F32); gk1 = A("gk1", [128, 128], F32); Gk4 = A("Gk4", [128, 512], F32)
            P.dma("sp", gm[:], mem_norm_g[l].partition_broadcast(128), writes=[gm])
            P.dma("sp", gk1[:], mem_k_norm[l].partition_broadcast(128), writes=[gk1])
            CP(Gk4[:].rearrange("p (h d) -> p h d", h=4), gk1[:].unsqueeze(1).broadcast_to([128, 4, 128]), [gk1], [Gk4], "pool")
            MkT = A("MkT", [128, 4, 256], BF16); Mv = A("Mv", [128, 2, 4, 128], BF16)
            xt = A("mxt", [128, 1024], F32); junk = A("mjunk", [128, 1024], BF16); ss = A("mss", [128, 1], F32)
            h = A("mh", [128, 1024], BF16); hT = A("mhT", [128, 8, 128], BF16)
            xs = A("mxs", [128, 512], F32); sq = A("msq", [128, 512], F32); ssq = A("mssq", [128, 4], F32); kn = A("mkn", [128, 512], BF16)
            v3 = lambda ap: ap.rearrange("p (h d) -> p h d", h=4)
            for mt in range(2):
                P.dma("sp", xt[:], mem_in[mt * 128:(mt + 1) * 128, :], writes=[xt])
                rmsnorm_rows(A, xt, gm, h, junk, ss)
                transpose8(h, hT)
                b = nb()
                MM(b, [(b[:, :], hT[:, kc, :], Wm[:, kc, 0:512], kc == 0, kc == 7) for kc in range(8)], [hT, Wm])
                ACT(xs[:], b[:, :], AF.Copy, [b], [xs])
                ACT(sq[:], b[:, :], AF.Square, [b], [sq])
                P.op("dve", lambda e: e.tensor_reduce(out=ssq[:], in_=v3(sq[:]), axis=AX.X, op=ALU.add), [sq], [ssq])
                TS(ssq[:], ssq[:], 1.0 / 128, EPS, ALU.mult, ALU.add, [ssq], [ssq])
                ACT(ssq[:], ssq[:], AF.Sqrt, [ssq], [ssq])
                RECIP(ssq[:], ssq[:], [ssq], [ssq])
                TT(v3(xs[:]), v3(xs[:]), ssq[:].unsqueeze(2).broadcast_to([128, 4, 128]), ALU.mult, [xs, ssq], [xs])
                TT(kn[:], xs[:], Gk4[:], ALU.mult, [xs, Gk4], [kn])
                b = nb()
                bv = b[:].bitcast(BF16).rearrange("p (c t) -> p c t", c=8)
                TR(b, [(bv[:, j, :], kn[:, j * 128:(j + 1) * 128], identb[:]) for j in range(4)], [kn, identb])
                CP(MkT[:, :, mt * 128:(mt + 1) * 128], bv[:, 0:4, :], [b], [MkT])
                b = nb()
                MM(b, [(b[:, :], hT[:, kc, :], Wm[:, kc, 512:1024], kc == 0, kc == 7) for kc in range(8)], [hT, Wm])
                CP(Mv[:, mt, :, :], v3(b[:, :]), [b], [Mv])
            MqTs = [A("MqT%d" % i, [128, T], BF16) for i in range(2)]
            PTs = [A("mPT%d" % i, [128, 512], BF16) for i in range(3)]
            rv = A("mrv", [128, 512], F32); t1 = A("mt1", [128, 512], F32)
            siTs = [A("msiT%d" % i, [128, 512], BF16) for i in range(2)]
            omTs = [A("omT%d" % i, [128, 512], BF16) for i in range(2)]
            pk = [0]; qi = 0; sct = [0]
            for hm in range(4):
                MqT = MqTs[hm % 2]
                P.dma("sp", MqT[:], FM[R_MQ + hm * 128:R_MQ + (hm + 1) * 128, :], writes=[MqT])
                for qg in range(T // 512):
                    qsl = slice(qg * 512, (qg + 1) * 512)
                    Bo = banks[4 + qi % 2]; Brs = banks[6 + qi % 2]; siT = siTs[qi % 2]; omT = omTs[qi % 2]; qi += 1
                    P.dma("sp", siT[:], FM[R_SM + hm * 128:R_SM + (hm + 1) * 128, qsl], writes=[siT])
                    for mc in range(2):
                        b = banks[sct[0] % 4]; sct[0] += 1
                        MM(b, [(b[:, :], MkT[:, hm, mc * 128:(mc + 1) * 128], MqT[:, qsl], True, True)], [MkT, MqT])
                        pt = PTs[pk[0] % 3]; pk[0] += 1
                        ACT(pt[:], b[:, :], AF.Exp, [b], [pt], scale=float(128 ** -0.5))
                        MM(Bo, [(Bo[:, :], Mv[:, mc, hm, :], pt[:], mc == 0, mc == 1)], [Mv, pt])
                        MM(Brs, [(Brs[:, :], onesb[:], pt[:], mc == 0, mc == 1)], [onesb, pt])
                    RECIP(rv[:], Brs[:, :], [Brs], [rv])
                    TT(t1[:], Bo[:, :], rv[:], ALU.mult, [Bo, rv], [t1])
                    TT(omT[:], t1[:], siT[:], ALU.mult, [t1, siT], [omT])
                    P.dma("pool", OT[1024 + hm * 128:1024 + (hm + 1) * 128, qsl], omT[:], reads=[omT])
            P.barrier()
            P.emit()

    def phase_E(l, x_src):
        with ExitStack() as es:
            A = lambda n, s, d: Buf(es.enter_context(nc.sbuf_tensor("%s_L%d" % (n, l), s, d)))
            stg = [A("estg%d" % i, [128, 2048], F32) for i in range(2)]
            cnt = [0]
            Wg = A("Wg", [128, 8, 3072], BF16); Wb3 = A("Wb3", [128, 12, 1024], BF16); Wo = A("Wo", [128, 8, 1024], BF16)
            for kc in range(8):
                ksl = slice(kc * 128, (kc + 1) * 128)
                load_cast(stg, cnt, Wg, Wg[:, kc, 0:2048], w_in[l, ksl, 4896:6944], 2048)
                load_cast(stg, cnt, Wg, Wg[:, kc, 2048:3072], w_in[l, ksl, 6944:7968], 1024)
                load_cast(stg, cnt, Wo, Wo[:, kc, :], w_out[l, ksl, :], 1024)
            for xb in range(3):
                for kc in range(4):
                    load_cast(stg, cnt, Wb3, Wb3[:, 4 * xb + kc, :], w_br[xb][l, kc * 128:(kc + 1) * 128, :], 1024)
            gn = A("egn", [128, 1024], F32)
            P.dma("sp", gn[:], norm_g[l].partition_broadcast(128), writes=[gn])
            xts = [A("ext%d" % i, [128, 1024], F32) for i in range(2)]
            oTs = [A("eoT%d" % i, [128, 12, 128], BF16) for i in range(2)]
            junk = A("ejunk", [128, 1024], BF16); ss = A("ess", [128, 1], F32)
            h = A("eh", [128, 1024], BF16); hT = A("ehT", [128, 8, 128], BF16)
            sg = A("esg", [128, 3072], F32); u = A("eu", [128, 1024], F32); tmp = A("etmp", [128, 512], F32)
            ub = A("eub", [128, 1024], BF16); uT = A("euT", [128, 8, 128], BF16)
            xos = [A("exo%d" % i, [128, 1024], F32) for i in range(2)]
            for tt in range(NT):
                tsl = slice(tt * 128, (tt + 1) * 128)
                xt = xts[tt % 2]; oT = oTs[tt % 2]; xo = xos[tt % 2]
                P.dma("sp", xt[:], x_src[tsl, :], writes=[xt])
                P.dma("sp", oT[:], OT[:, tsl].rearrange("(c p) t -> p c t", p=128), writes=[oT])
                rmsnorm_rows(A, xt, gn, h, junk, ss)
                transpose8(h, hT)
                for cc in range(6):
                    b = nb()
                    MM(b, [(b[:, :], hT[:, kc, :], Wg[:, kc, cc * 512:(cc + 1) * 512], kc == 0, kc == 7) for kc in range(8)], [hT, Wg])
                    ACT(sg[:, cc * 512:(cc + 1) * 512], b[:, :], AF.Sigmoid, [b], [sg])
                for xb in range(3):
                    for cc in range(2):
                        csl = slice(cc * 512, (cc + 1) * 512)
                        b = nb()
                        MM(b, [(b[:, :], oT[:, 4 * xb + kc, :], Wb3[:, 4 * xb + kc, csl], kc == 0, kc == 3) for kc in range(4)], [oT, Wb3])
                        gsl = slice(xb * 1024 + cc * 512, xb * 1024 + (cc + 1) * 512)
                        if xb == 0:
                            TT(u[:, csl], b[:, :], sg[:, gsl], ALU.mult, [b, sg], [u])
                        else:
                            TT(tmp[:], b[:, :], sg[:, gsl], ALU.mult, [b, sg], [tmp])
                            TT(u[:, csl], u[:, csl], tmp[:], ALU.add, [u, tmp], [u])
                CP(ub[:], u[:], [u], [ub], "pool")
                transpose8(ub, uT)
                for cc in range(2):
                    csl = slice(cc * 512, (cc + 1) * 512)
                    b = nb()
                    MM(b, [(b[:, :], uT[:, kc, :], Wo[:, kc, csl], kc == 0, kc == 7) for kc in range(8)], [uT, Wo])
                    TT(xo[:, csl], b[:, :], xt[:, csl], ALU.add, [b, xt], [xo])
                P.dma("pool", out[tsl, :], xo[:], reads=[xo])
            P.barrier()
            P.emit()

    for l in range(L):
        xs_ = x_in if l == 0 else out
        if "P" in phases:
            phase_P(l, xs_)
        if "C" in phases:
            phase_C(l)
        if "1" in phases:
            phase_A1(l)
        if "2" in phases:
            phase_A2(l)
        if "3" in phases:
            phase_A3(l)
        if "E" in phases:
            phase_E(l, xs_)
    gs.close()
    return nc


def make_consts(T):
    NT = T // 128
    NCMP = T // 16 - 1
    NCH = (NCMP + 127) // 128
    f = np.float32
    half = 32
    inv = (10000.0 ** (-np.arange(half, dtype=np.float32) / half)).astype(np.float32)
    ang = np.arange(T, dtype=np.float32)[:, None] * inv[None, :]
    cos = np.cos(ang).astype(f); sin = np.sin(ang).astype(f)
    k = np.arange(128)
    tri = (k[:, None] <= k[None, :]).astype(f)
    atri = (k[:, None] > k[None, :]).astype(f)
    sel = np.zeros((128, 128), f); sel[63, :] = 1.0
    eall = np.zeros((128, NT, 128), f)
    for c in range(NT):
        eall[2 * c, c, :64] = 1.0
        eall[2 * c + 1, c, 64:] = 1.0
    cand = np.zeros((NT, 128, 128), f); addm = np.zeros((NT, 128, 128), f)
    j = np.arange(128)[None, :]
    for i in range(NT):
        cur = (2 * i + (k >= 64).astype(np.int64))[:, None]
        forced = (j == 0) | (j == cur) | (j == cur - 1)
        cnd = (j >= 1) & (j <= cur - 2)
        cand[i] = cnd.astype(f)
        addm[i] = np.where(forced, 8.0, np.where(cnd, 0.0, -1.0)).astype(f)
    n_slc = T // 64
    cs = np.arange(NCMP)[:, None] * 16
    ss = np.arange(128)[None, :] * 64
    ov = np.clip(np.minimum(cs + 32, ss + 64) - np.maximum(cs, ss), 0, None).astype(f) / 32.0
    ov[:, n_slc:] = 0.0
    ovp = np.zeros((NCH * 128, 128), f); ovp[:NCMP] = ov
    m8 = (k[:, None] >= 16 * np.arange(8)[None, :] + 15).astype(f)
    return dict(c_ident=np.eye(128, dtype=f), c_ropec=np.concatenate([cos, cos], 1), c_ropes=np.concatenate([-sin, sin], 1),
                c_tri=tri, c_atri=atri, c_sel=sel, c_eall=eall.reshape(128, NT * 128), c_cand=cand, c_add=addm, c_ov=ovp, c_mask8=m8)


WNAMES = ["norm_g", "mem_norm_g", "w_in", "nsa_q_norm", "nsa_k_norm", "cmp_pe_k", "cmp_w1_k", "cmp_w2_k", "cmp_pe_v", "cmp_w1_v",
          "cmp_w2_v", "fox_q_norm", "fox_k_norm", "fox_f_bias", "mem_q_norm", "mem_k_norm", "w_mem_kv", "w_branch_a", "w_branch_b",
          "w_branch_m", "w_out"]


def run(inputs, T, L, dbg=False, n_cores=8, phases="PC123E", trace=False):
    nc = bass.Bass("TRN2", target_bir_lowering=False)
    build(nc, T, L, dbg, phases)
    consts = make_consts(T)
    B = inputs["x"].shape[0]
    in_maps = []
    for c in range(n_cores):
        b = c % B
        m = {"x": np.ascontiguousarray(inputs["x"][b], dtype=np.float32), "mem": np.ascontiguousarray(inputs["mem"][b], dtype=np.float32)}
        for n in WNAMES:
            m[n] = np.ascontiguousarray(inputs[n][:L], dtype=np.float32)
        m.update(consts)
        in_maps.append(m)
    res = run_bass_kernel_spmd(nc, in_maps, core_ids=list(range(n_cores)), **({"trace": True} if trace else {}))
    return res


def kernel(**inputs):
    inputs = {k: np.asarray(v) for k, v in inputs.items()}
    res = run(inputs, 8192, 4)
    B = inputs["x"].shape[0]
    return np.stack([np.asarray(res.results[b]["out"], dtype=np.float32) for b in range(B)], axis=0)
```

```python
import numpy as np
import concourse.bass as bass
import concourse.mybir as mybir

F32 = mybir.dt.float32
BF16 = mybir.dt.bfloat16
AF = mybir.ActivationFunctionType
ALU = mybir.AluOpType
AX = mybir.AxisListType


class Tok:
    __slots__ = ("w", "r")

    def __init__(self):
        self.w = None
        self.r = {}


class Buf:
    def __init__(self, t):
        self.t = t
        self.tok = Tok()

    def __getitem__(self, idx):
        return self.t[idx]


NDS = 8


class Prog:
    def __init__(self, nc):
        self.nc = nc
        self.eng = {"pe": nc.tensor, "act": nc.scalar, "dve": nc.vector, "pool": nc.gpsimd, "sp": nc.sync}
        self.ops = {k: [] for k in self.eng}
        self.cnt = {k: 0 for k in self.eng}
        self.sem = {k: nc.alloc_semaphore("sem_" + k) for k in self.eng}
        self.dsem = {k: [nc.alloc_semaphore("dq_%s_%d" % (k, i)) for i in range(NDS)] for k in ("sp", "pool", "act")}
        self.dcnt = {k: 0 for k in self.dsem}
        self.waited = {}
        self.semobj = {}
        self.n_ops = 0

    def _waits(self, eng, evs):
        best = {}
        for ev in evs:
            if ev is None:
                continue
            key, val = ev
            if self.waited.get((eng, key), 0) >= val:
                continue
            if best.get(key, 0) < val:
                best[key] = val
        out = []
        for key, val in best.items():
            self.waited[(eng, key)] = val
            out.append((self.semobj[key], val))
        return out

    def _deps(self, reads, writes):
        evs = []
        for b in reads:
            evs.append(b.tok.w)
        for b in writes:
            evs.append(b.tok.w)
            evs.extend(b.tok.r.values())
        return evs

    def _commit(self, ev, reads, writes, rkey):
        for b in reads:
            b.tok.r[rkey] = ev
        for b in writes:
            b.tok.w = ev
            b.tok.r = {}

    def op(self, eng, fn, reads=(), writes=()):
        evs = self._deps(reads, writes)
        self.cnt[eng] += 1
        key = "c_" + eng
        self.semobj[key] = self.sem[eng]
        ev = (key, self.cnt[eng])
        waits = self._waits(eng, evs)
        self.ops[eng].append((waits, fn, self.sem[eng], 1))
        self._commit(ev, reads, writes, key)
        self.n_ops += 1
        return ev

    def dma(self, q, out, in_, reads=(), writes=()):
        evs = self._deps(reads, writes)
        m = self.dcnt[q]
        self.dcnt[q] += 1
        slot = m % NDS
        key = "d_%s_%d" % (q, slot)
        self.semobj[key] = self.dsem[q][slot]
        prev = (key, 16 * (m // NDS)) if m >= NDS else None
        ev = (key, 16 * (m // NDS + 1))
        waits = self._waits(q, evs + [prev])
        self.ops[q].append((waits, lambda e, o=out, i=in_: e.dma_start(out=o, in_=i), self.dsem[q][slot], 16))
        self._commit(ev, reads, writes, "dq_%s_%d" % (q, m))
        self.n_ops += 1
        return ev

    def barrier(self, extra=()):
        evs = []
        for k in self.eng:
            if self.cnt[k]:
                self.semobj["c_" + k] = self.sem[k]
                evs.append(("c_" + k, self.cnt[k]))
        for q in self.dsem:
            m = self.dcnt[q]
            for s in range(NDS):
                n = (m - s + NDS - 1) // NDS if m > s else 0
                if n:
                    evs.append(("d_%s_%d" % (q, s), 16 * n))
        for k in self.eng:
            w = self._waits(k, evs)
            if w:
                self.ops[k].append((w, None, None, 0))

    def emit(self):
        nc = self.nc
        ops = self.ops
        with nc.Block() as block:
            def run(e, lst):
                for waits, fn, sem, inc in lst:
                    for s, v in waits:
                        e.wait_ge(s, v)
                    if fn is not None:
                        fn(e).then_inc(sem, inc)

            @block.tensor
            def _(e):
                run(e, ops["pe"])

            @block.scalar
            def _(e):
                run(e, ops["act"])

            @block.vector
            def _(e):
                run(e, ops["dve"])

            @block.gpsimd
            def _(e):
                run(e, ops["pool"])

            @block.sync
            def _(e):
                run(e, ops["sp"])
        self.ops = {k: [] for k in self.eng}

from contextlib import ExitStack
from concourse.bass_utils import run_bass_kernel_spmd
import ml_dtypes

EPS = 1e-6
BIG = 30000.0
CH = [(0, 512), (512, 512), (1024, 288), (1312, 512), (1824, 512), (2336, 512), (2848, 512),
      (3360, 512), (3872, 512), (4384, 512)]
WPIECES = [(0, 0, 1304), (1304, 3352, 8), (1312, 1304, 2048), (3360, 3360, 1536)]
R_QN, R_KC, R_VC, R_KS, R_KW, R_FQ, R_FK, R_MQ, R_SA, R_SB, R_SM = 0, 512, 640, 768, 896, 1024, 1536, 2048, 2560, 3072, 3584


def build(nc, T, L, dbg=False, phases="PC123E"):
    NT = T // 128
    NCMP = T // 16 - 1
    NCH = (NCMP + 127) // 128
    P = Prog(nc)
    din = lambda n, s, d=F32: nc.dram_tensor(n, s, d, kind="ExternalInput").ap()
    x_in = din("x", [T, 1024]); mem_in = din("mem", [256, 1024])
    norm_g = din("norm_g", [L, 1024]); mem_norm_g = din("mem_norm_g", [L, 1024])
    w_in = din("w_in", [L, 1024, 7968])
    nsa_q_norm = din("nsa_q_norm", [L, 64]); nsa_k_norm = din("nsa_k_norm", [L, 3, 64])
    cmp_pe = [din("cmp_pe_k", [L, 32, 64]), din("cmp_pe_v", [L, 32, 64])]
    cmp_w1 = [din("cmp_w1_k", [L, 2048, 128]), din("cmp_w1_v", [L, 2048, 128])]
    cmp_w2 = [din("cmp_w2_k", [L, 128, 64]), din("cmp_w2_v", [L, 128, 64])]
    fox_q_norm = din("fox_q_norm", [L, 64]); fox_k_norm = din("fox_k_norm", [L, 64]); fox_f_bias = din("fox_f_bias", [L, 8])
    mem_q_norm = din("mem_q_norm", [L, 128]); mem_k_norm = din("mem_k_norm", [L, 128])
    w_mem_kv = din("w_mem_kv", [L, 1024, 1024])
    w_br = [din("w_branch_a", [L, 512, 1024]), din("w_branch_b", [L, 512, 1024]), din("w_branch_m", [L, 512, 1024])]
    w_out = din("w_out", [L, 1024, 1024])
    c_ident = din("c_ident", [128, 128]); c_ropec = din("c_ropec", [T, 64]); c_ropes = din("c_ropes", [T, 64])
    c_tri = din("c_tri", [128, 128]); c_atri = din("c_atri", [128, 128]); c_sel = din("c_sel", [128, 128])
    c_eall = din("c_eall", [128, NT * 128]); c_cand = din("c_cand", [NT, 128, 128]); c_add = din("c_add", [NT, 128, 128])
    c_ov = din("c_ov", [NCH * 128, 128]); c_mask8 = din("c_mask8", [128, 8])
    out = nc.dram_tensor("out", [T, 1024], F32, kind="ExternalOutput").ap()
    sk = "ExternalOutput" if dbg else "Internal"
    FM = nc.dram_tensor("FM", [4096, T], BF16, kind=sk).ap()
    GT = nc.dram_tensor("GT", [32, T], F32, kind=sk).ap()
    TMV = nc.dram_tensor("TMV", [T, 12, 65], BF16, kind=sk).ap()
    OT = nc.dram_tensor("OT", [1536, T], BF16, kind=sk).ap()
    DBG = nc.dram_tensor("DBG", [3, 512, T], F32, kind=sk).ap() if dbg else None

    banks = [Buf(nc.alloc_psum_tensor("bank%d" % i, [128, 512], F32)) for i in range(8)]
    bctr = [0]

    def nb(lo=0, hi=8):
        b = banks[lo + bctr[0] % (hi - lo)]
        bctr[0] += 1
        return b

    gs = ExitStack()
    GA = lambda n, s, d: Buf(gs.enter_context(nc.sbuf_tensor(n, s, d)))
    identf = GA("identf", [128, 128], F32); identb = GA("identb", [128, 128], BF16)
    trib = GA("trib", [128, 128], BF16); atrib = GA("atrib", [128, 128], BF16)
    trif = GA("trif", [128, 128], F32); self_ = GA("self", [128, 128], F32); onesf = GA("onesf", [128, 128], F32)
    onesb = GA("onesb", [128, 128], BF16)
    mask8 = GA("mask8", [128, 8], F32)
    tmpf = GA("tmpf", [128, 128], F32)
    LF = GA("LF", [128, NT, 8], F32)
    KcT = [GA("KcT%d" % g, [64, NCH * 128], BF16) for g in range(2)]
    Vc = [GA("Vc%d" % g, [128, NCH, 64], BF16) for g in range(2)]

    def ACT(out_, in_, func, reads, writes, **kw):
        return P.op("act", lambda e: e.activation(out=out_, in_=in_, func=func, **kw), reads, writes)

    def TT(out_, a, b, op, reads, writes, eng="dve"):
        return P.op(eng, lambda e: e.tensor_tensor(out=out_, in0=a, in1=b, op=op), reads, writes)

    def TS(out_, a, s1, s2, op0, op1, reads, writes, eng="dve"):
        if op1 is None:
            return P.op(eng, lambda e: e.tensor_scalar(out=out_, in0=a, scalar1=s1, scalar2=None, op0=op0), reads, writes)
        return P.op(eng, lambda e: e.tensor_scalar(out=out_, in0=a, scalar1=s1, scalar2=s2, op0=op0, op1=op1), reads, writes)

    def CP(out_, in_, reads, writes, eng="dve"):
        return P.op(eng, lambda e: e.tensor_copy(out=out_, in_=in_), reads, writes)

    def MM(outb, specs, reads):
        def fn(e):
            for (o, l, r, st, sp) in specs:
                i = e.matmul(o, lhsT=l, rhs=r, start=st, stop=sp)
            return i
        return P.op("pe", fn, reads, outb if isinstance(outb, list) else [outb])

    def TR(outb, specs, reads):
        def fn(e):
            for (o, i_, idn) in specs:
                i = e.transpose(out=o, in_=i_, identity=idn)
            return i
        return P.op("pe", fn, reads, [outb])

    def RECIP(out_, in_, reads, writes):
        return P.op("dve", lambda e: e.reciprocal(out=out_, in_=in_), reads, writes)

    P.dma("sp", identf[:], c_ident[:, :], writes=[identf])
    CP(identb[:], identf[:], [identf], [identb], "pool")
    P.dma("sp", trif[:], c_tri[:, :], writes=[trif])
    CP(trib[:], trif[:], [trif], [trib], "pool")
    P.dma("sp", tmpf[:], c_atri[:, :], writes=[tmpf])
    CP(atrib[:], tmpf[:], [tmpf], [atrib], "pool")
    P.dma("sp", self_[:], c_sel[:, :], writes=[self_])
    P.dma("sp", mask8[:], c_mask8[:, :], writes=[mask8])
    P.op("pool", lambda e: e.memset(onesf[:], 1.0), [], [onesf])
    P.op("pool", lambda e: e.memset(onesb[:], 1.0), [], [onesb])

    def rmsnorm_rows(A, xt, gn, h, junk, ss):
        ACT(junk[:], xt[:], AF.Square, [xt], [junk, ss], accum_out=ss[:])
        TS(ss[:], ss[:], 1.0 / 1024, EPS, ALU.mult, ALU.add, [ss], [ss])
        ACT(ss[:], ss[:], AF.Sqrt, [ss], [ss])
        RECIP(ss[:], ss[:], [ss], [ss])
        P.op("dve", lambda e: e.scalar_tensor_tensor(out=h[:], in0=xt[:], scalar=ss[:, 0:1], in1=gn[:], op0=ALU.mult, op1=ALU.mult),
             [xt, ss, gn], [h])

    def transpose8(h, hT):
        b = nb()
        bv = b[:].bitcast(BF16).rearrange("p (c t) -> p c t", c=8)
        TR(b, [(bv[:, c, :], h[:, c * 128:(c + 1) * 128], identb[:]) for c in range(8)], [h, identb])
        ACT(hT[:], bv, AF.Copy, [b], [hT])

    def load_cast(stg, cnt, dst, dst_ap, src_ap, w, parts=128):
        s = stg[cnt[0] % len(stg)]
        cnt[0] += 1
        P.dma("sp", s[0:parts, 0:w], src_ap, writes=[s])
        CP(dst_ap, s[0:parts, 0:w], [s], [dst], "pool")

    class Pipe:
        def __init__(self, depth=2):
            self.q = []
            self.depth = depth

        def step(self, s_fn, mid_fn, pv_fn):
            s_fn()
            if len(self.q) >= self.depth:
                self.q.pop(0)()
            mid_fn()
            self.q.append(pv_fn)

        def flush(self):
            while self.q:
                self.q.pop(0)()

    def phase_P(l, x_src):
        with ExitStack() as es:
            A = lambda n, s, d: Buf(es.enter_context(nc.sbuf_tensor("%s_L%d" % (n, l), s, d)))
            Wp = A("Wp", [128, 8, 4896], BF16)
            stg = [A("stg%d" % i, [128, 2048], F32) for i in range(2)]
            cnt = [0]
            for kc in range(8):
                for (d0, s0, w) in WPIECES:
                    load_cast(stg, cnt, Wp, Wp[:, kc, d0:d0 + w], w_in[l, kc * 128:(kc + 1) * 128, s0:s0 + w], w)
            g1 = A("g1", [128, 64 * 6 + 128 + 8], F32)
            srcs = [nsa_q_norm[l], nsa_k_norm[l, 0], nsa_k_norm[l, 1], nsa_k_norm[l, 2], fox_q_norm[l], fox_k_norm[l]]
            for i, s in enumerate(srcs):
                P.dma("sp", g1[:, i * 64:(i + 1) * 64], s.partition_broadcast(128), writes=[g1])
            P.dma("sp", g1[:, 384:512], mem_q_norm[l].partition_broadcast(128), writes=[g1])
            P.dma("sp", g1[:, 512:520], fox_f_bias[l].partition_broadcast(128), writes=[g1])
            gn = A("gn", [128, 1024], F32)
            P.dma("sp", gn[:], norm_g[l].partition_broadcast(128), writes=[gn])
            Gq = A("Gq", [128, 512], F32); Gks = A("Gks", [128, 128], F32); Gkw = A("Gkw", [128, 128], F32)
            Gfq = A("Gfq", [128, 512], F32); Gfk = A("Gfk", [128, 512], F32); Gmq = A("Gmq", [128, 512], F32)

            def expand(G, H, D, off):
                CP(G[:].rearrange("p (h d) -> p h d", h=H), g1[:, off:off + D].unsqueeze(1).broadcast_to([128, H, D]), [g1], [G], "pool")
            expand(Gq, 8, 64, 0); expand(Gks, 2, 64, 128); expand(Gkw, 2, 64, 192)
            expand(Gfq, 8, 64, 256); expand(Gfk, 8, 64, 320); expand(Gmq, 4, 128, 384)
            xts = [A("xt%d" % i, [128, 1024], F32) for i in range(2)]
            css = [A("cs%d" % i, [128, 64], F32) for i in range(2)]
            sns = [A("sn%d" % i, [128, 64], F32) for i in range(2)]
            junk = A("junk", [128, 1024], BF16); ss = A("ss", [128, 1], F32)
            h = A("h", [128, 1024], BF16); hT = A("hT", [128, 8, 128], BF16)
            scr = [dict(sq=A("sq%d" % i, [128, 512], F32), xs=A("xs%d" % i, [128, 512], F32), ssq=A("ssq%d" % i, [128, 8], F32),
                        ta=A("ta%d" % i, [128, 512], F32), tb=A("tb%d" % i, [128, 512], F32)) for i in range(2)]
            sctr = [0]
            zb = [A("zb%d" % i, [128, 512], BF16) for i in range(8)]
            stage = A("stage", [128, 32, 128], BF16)
            vts = [A("vt%d" % i, [128, 12, 65], BF16) for i in range(2)]
            for v in vts:
                P.op("pool", lambda e, v=v: e.memset(v[:], 1.0), [], [v])
            gtt = A("gtt", [128, 32], F32); gts = A("gts", [32, 128], F32)
            ft = A("ft", [128, 8], F32); sgm = A("sgm", [128, 512], F32)

            def norm_rope(psb, ps_ap, W, H, D, G, rope, dst, dst_ap, cs, sn):
                S = scr[sctr[0] % 2]
                sctr[0] += 1
                sq, xs, ssq, ta, tb = S["sq"], S["xs"], S["ssq"], S["ta"], S["tb"]
                v3 = lambda ap: ap.rearrange("p (h d) -> p h d", h=H)
                ACT(xs[:, :W], ps_ap, AF.Copy, [psb], [xs])
                if G is not None:
                    ACT(sq[:, :W], ps_ap, AF.Square, [psb], [sq])
                    P.op("dve", lambda e: e.tensor_reduce(out=ssq[:, :H], in_=v3(sq[:, :W]), axis=AX.X, op=ALU.add), [sq], [ssq])
                    TS(ssq[:, :H], ssq[:, :H], 1.0 / D, EPS, ALU.mult, ALU.add, [ssq], [ssq])
                    ACT(ssq[:, :H], ssq[:, :H], AF.Sqrt, [ssq], [ssq])
                    RECIP(ssq[:, :H], ssq[:, :H], [ssq], [ssq])
                    TT(v3(xs[:, :W]), v3(xs[:, :W]), ssq[:, :H].unsqueeze(2).broadcast_to([128, H, D]), ALU.mult, [xs, ssq], [xs])
                    if not rope:
                        TT(dst_ap, xs[:, :W], G[:, :W], ALU.mult, [xs, G], [dst])
                        return
                    TT(xs[:, :W], xs[:, :W], G[:, :W], ALU.mult, [xs, G], [xs])
                x4 = xs[:, :W].rearrange("p (h two e) -> p h two e", h=H, two=2)
                t4 = tb[:, :W].rearrange("p (h two e) -> p h two e", h=H, two=2)
                TT(v3(ta[:, :W]), v3(xs[:, :W]), cs[:].unsqueeze(1).broadcast_to([128, H, 64]), ALU.mult, [xs, cs], [ta])
                TT(t4[:, :, 0, :], x4[:, :, 1, :], sn[:, 0:32].unsqueeze(1).broadcast_to([128, H, 32]), ALU.mult, [xs, sn], [tb])
                TT(t4[:, :, 1, :], x4[:, :, 0, :], sn[:, 32:64].unsqueeze(1).broadcast_to([128, H, 32]), ALU.mult, [xs, sn], [tb])
                TT(dst_ap, ta[:, :W], tb[:, :W], ALU.add, [ta, tb], [dst])

            def silu_to(psb, ps_ap, dst, dst_ap):
                ACT(sgm[:], ps_ap, AF.Sigmoid, [psb], [sgm])
                TT(dst_ap, ps_ap, sgm[:], ALU.mult, [psb, sgm], [dst])

            for tt in range(NT):
                xt = xts[tt % 2]; cs = css[tt % 2]; sn = sns[tt % 2]; vt = vts[tt % 2]
                tsl = slice(tt * 128, (tt + 1) * 128)
                P.dma("sp", xt[:], x_src[tsl, :], writes=[xt])
                P.dma("sp", cs[:], c_ropec[tsl, :], writes=[cs])
                P.dma("sp", sn[:], c_ropes[tsl, :], writes=[sn])
                rmsnorm_rows(A, xt, gn, h, junk, ss)
                transpose8(h, hT)
                for cc in range(10):
                    off, w = CH[cc]
                    b = nb()
                    MM(b, [(b[:, 0:w], hT[:, kc, :], Wp[:, kc, off:off + w], kc == 0, kc == 7) for kc in range(8)], [hT, Wp])
                    if cc == 0:
                        norm_rope(b, b[:, 0:512], 512, 8, 64, Gq, True, zb[0], zb[0][:, :], cs, sn)
                    elif cc == 1:
                        norm_rope(b, b[:, 0:128], 128, 2, 64, None, True, zb[1], zb[1][:, 0:128], cs, sn)
                        ACT(zb[1][:, 128:256], b[:, 128:256], AF.Copy, [b], [zb[1]])
                        norm_rope(b, b[:, 256:384], 128, 2, 64, Gks, True, zb[1], zb[1][:, 256:384], cs, sn)
                        ACT(vt[:, 0:2, 0:64], b[:, 384:512].rearrange("p (h d) -> p h d", h=2), AF.Copy, [b], [vt])
                    elif cc == 2:
                        norm_rope(b, b[:, 0:128], 128, 2, 64, Gkw, True, zb[1], zb[1][:, 384:512], cs, sn)
                        ACT(vt[:, 2:4, 0:64], b[:, 128:256].rearrange("p (h d) -> p h d", h=2), AF.Copy, [b], [vt])
                        ACT(gtt[:, 0:24], b[:, 256:280], AF.Sigmoid, [b], [gtt])
                        TT(ft[:], b[:, 280:288], g1[:, 512:520], ALU.add, [b, g1], [ft])
                        ACT(ft[:], ft[:], AF.Sigmoid, [ft], [ft])
                        ACT(LF[:, tt, :], ft[:], AF.Ln, [ft], [LF])
                    elif cc == 3:
                        silu_to(b, b[:, 0:512], zb[5], zb[5][:, :])
                    elif cc == 4:
                        norm_rope(b, b[:, 0:512], 512, 8, 64, Gfq, False, zb[2], zb[2][:, :], cs, sn)
                    elif cc == 5:
                        norm_rope(b, b[:, 0:512], 512, 8, 64, Gfk, False, zb[3], zb[3][:, :], cs, sn)
                    elif cc == 6:
                        ACT(vt[:, 4:12, 0:64], b[:, 0:512].rearrange("p (h d) -> p h d", h=8), AF.Copy, [b], [vt])
                    elif cc == 7:
                        silu_to(b, b[:, 0:512], zb[6], zb[6][:, :])
                    elif cc == 8:
                        norm_rope(b, b[:, 0:512], 512, 4, 128, Gmq, False, zb[4], zb[4][:, :], cs, sn)
                    elif cc == 9:
                        silu_to(b, b[:, 0:512], zb[7], zb[7][:, :])
                for pr in range(4):
                    b = nb()
                    bv = b[:].bitcast(BF16).rearrange("p (c t) -> p c t", c=8)
                    specs = []
                    for j in range(8):
                        z = zb[pr * 2 + j // 4]
                        specs.append((bv[:, j, :], z[:, (j % 4) * 128:(j % 4 + 1) * 128], identb[:]))
                    TR(b, specs, [zb[pr * 2], zb[pr * 2 + 1], identb])
                    if pr % 2 == 0:
                        ACT(stage[:, pr * 8:(pr + 1) * 8, :], bv, AF.Copy, [b], [stage])
                    else:
                        CP(stage[:, pr * 8:(pr + 1) * 8, :], bv, [b], [stage])
                P.dma("pool", FM[:, tsl].rearrange("(b p) t -> p b t", p=128), stage[:], reads=[stage])
                P.dma("pool", TMV[tsl, :, :], vt[:], reads=[vt])
                b = nb()
                TR(b, [(b[0:24, 0:128], gtt[:, 0:24], identf[:])], [gtt, identf])
                CP(gts[0:24, :], b[0:24, 0:128], [b], [gts])
                P.dma("pool", GT[0:24, tsl], gts[0:24, :], reads=[gts])
            P.barrier()
            P.emit()

    def phase_C(l):
        with ExitStack() as es:
            A = lambda n, s, d: Buf(es.enter_context(nc.sbuf_tensor("%s_L%d" % (n, l), s, d)))
            XTs = [A("XT%d" % i, [64, T], BF16) for i in range(2)]
            stg = [A("cstg%d" % i, [128, 2048], F32) for i in range(2)]
            cnt = [0]
            w1b = A("w1b", [64, 32, 128], BF16); w2b = A("w2b", [128, 64], BF16)
            pef = A("pef", [32, 64], F32); peT = A("peT", [64, 32], BF16); cst = A("cst", [128, 1], F32)
            hidT = A("hidT", [128, 512], BF16); sgc = A("sgc", [128, 512], F32)
            g0 = A("g0", [128, 64], F32); kn = A("kn", [128, 64], BF16)
            ssq = A("cssq", [128, 1], F32); junk = A("cjunk", [128, 64], F32)
            P.dma("sp", g0[:], nsa_k_norm[l, 0].partition_broadcast(128), writes=[g0])
            xi = 0
            for kv in range(2):
                w1v = cmp_w1[kv][l].rearrange("(l d) h -> d l h", d=64)
                for hf in range(2):
                    s = stg[cnt[0] % 2]; cnt[0] += 1
                    P.dma("sp", s[0:64, :].rearrange("p (l h) -> p l h", l=16), w1v[:, hf * 16:(hf + 1) * 16, :], writes=[s])
                    CP(w1b[:, hf * 16:(hf + 1) * 16, :], s[0:64, :].rearrange("p (l h) -> p l h", l=16), [s], [w1b], "pool")
                load_cast(stg, cnt, w2b, w2b[:], cmp_w2[kv][l], 64)
                P.dma("sp", pef[:], cmp_pe[kv][l], writes=[pef])
                b = nb()
                TR(b, [(b[0:64, 0:32], pef[:], identf[0:32, 0:32])], [pef, identf])
                CP(peT[:], b[0:64, 0:32], [b], [peT])
                b = nb()
                MM(b, [(b[:, 0:1], w1b[:, l_, :], peT[:, l_:l_ + 1], l_ == 0, l_ == 31) for l_ in range(32)], [w1b, peT])
                CP(cst[:], b[:, 0:1], [b], [cst])
                for g in range(2):
                    XT = XTs[xi % 2]; xi += 1
                    row = (R_KC if kv == 0 else R_VC) + g * 64
                    P.dma("sp", XT[:], FM[row:row + 64, :], writes=[XT])
                    X3 = XT[:, :].rearrange("p (n s) -> p n s", s=16)
                    b = nb()
                    MM(b, [(b[:, 0:NCMP], w1b[:, l_, :], X3[:, l_ // 16:l_ // 16 + NCMP, l_ % 16], l_ == 0, l_ == 31) for l_ in range(32)],
                       [w1b, XT])
                    ACT(sgc[:, 0:NCMP], b[:, 0:NCMP], AF.Sigmoid, [b, cst], [sgc], bias=cst[:, 0:1])
                    P.op("dve", lambda e, b=b: e.scalar_tensor_tensor(out=hidT[:, 0:NCMP], in0=b[:, 0:NCMP], scalar=cst[:, 0:1], in1=sgc[:, 0:NCMP],
                                                                    op0=ALU.add, op1=ALU.mult), [b, cst, sgc], [hidT])
                    for c in range(NCH):
                        rows = min(128, NCMP - c * 128)
                        b2 = nb()
                        MM(b2, [(b2[0:rows, 0:64], hidT[:, c * 128:c * 128 + rows], w2b[:], True, True)], [hidT, w2b])
                        if kv == 0:
                            ACT(junk[0:rows, :], b2[0:rows, 0:64], AF.Square, [b2], [junk, ssq], accum_out=ssq[0:rows, :])
                            TS(ssq[0:rows, :], ssq[0:rows, :], 1.0 / 64, EPS, ALU.mult, ALU.add, [ssq], [ssq])
                            ACT(ssq[0:rows, :], ssq[0:rows, :], AF.Sqrt, [ssq], [ssq])
                            RECIP(ssq[0:rows, :], ssq[0:rows, :], [ssq], [ssq])
                            P.op("dve", lambda e, b2=b2, rows=rows: e.scalar_tensor_tensor(out=kn[0:rows, :], in0=b2[0:rows, 0:64], scalar=ssq[0:rows, 0:1],
                                                                                          in1=g0[0:rows, :], op0=ALU.mult, op1=ALU.mult), [b2, ssq, g0], [kn])
                            b3 = nb()
                            bv = b3[:].bitcast(BF16)
                            TR(b3, [(bv[0:64, 0:rows], kn[0:rows, :], identb[0:rows, 0:rows])], [kn, identb])
                            CP(KcT[g][:, c * 128:c * 128 + rows], bv[0:64, 0:rows], [b3], [KcT[g]])
                        else:
                            CP(Vc[g][0:rows, c, :], b2[0:rows, 0:64], [b2], [Vc[g]])
            P.barrier()
            P.emit()

    def phase_A1(l):
        with ExitStack() as es:
            A = lambda n, s, d: Buf(es.enter_context(nc.sbuf_tensor("%s_L%d" % (n, l), s, d)))
            stg = [A("astg%d" % i, [128, 1024], F32) for i in range(2)]
            cnt = [0]
            Eall = A("Eall", [128, NT, 128], BF16)
            Ef = Eall[:].rearrange("p c k -> p (c k)")
            for o in range(0, NT * 128, 1024):
                load_cast(stg, cnt, Eall, Ef[:, o:o + 1024], c_eall[:, o:o + 1024], 1024)
            ovb = A("ovb", [128, NCH, 128], BF16)
            for c in range(NCH):
                load_cast(stg, cnt, ovb, ovb[:, c, :], c_ov[c * 128:(c + 1) * 128, :], 128)
            QT4 = A("QT4", [64, 4, T], BF16); KsT = A("KsT", [64, T], BF16); KwT = A("KwT", [64, T], BF16)
            Vs = A("Vs", [128, NT, 65], BF16); Vw = A("Vw", [128, NT, 65], BF16)
            cands = [A("cand%d" % i, [128, 128], F32) for i in range(2)]
            adds = [A("addm%d" % i, [128, 128], F32) for i in range(2)]
            Es = [A("E%d" % i, [128, 512], F32) for i in range(2)]
            Pns = [A("Pn%d" % i, [128, 512], BF16) for i in range(2)]
            PnT = [A("PnT%d" % i, [128, 4, 128], BF16) for i in range(4)]
            rss = [A("rs%d" % i, [128, 1], F32) for i in range(2)]
            sc = A("sc", [128, 128], F32); scr = A("scr", [128, 128], F32)
            m8a = A("m8a", [128, 8], F32); m8b = A("m8b", [128, 8], F32); thr = A("thr", [128, 1], F32)
            M1 = A("M1", [128, 128], BF16); MT4 = A("MT4", [128, 4, 128], BF16)
            PTs = [A("PT%d" % i, [128, 512], BF16) for i in range(6)]
            pk = [0]
            gl = A("gl", [65, 3, 512], F32); Fr = A("Fr", [65, 3, 512], F32); Rsb = A("Rsb", [64, 3, 512], F32)
            t1 = A("t1", [64, 512], F32); t2 = A("t2", [64, 512], F32)
            siTs = [A("siT%d" % i, [64, 4, 128], BF16) for i in range(2)]
            oaTs = [A("oaT%d" % i, [64, 4, 128], BF16) for i in range(2)]
            Bos, Bow, Boc, Bimp = banks[3], banks[4], banks[5], banks[7]
            sct = [0]

            def nbS():
                b = banks[(0, 1, 2, 6)[sct[0] % 4]]
                sct[0] += 1
                return b
            tri4 = trib[:].unsqueeze(1).broadcast_to([128, 4, 128])
            atri4 = atrib[:].unsqueeze(1).broadcast_to([128, 4, 128])
            v4 = lambda ap: ap.rearrange("p (r q) -> p r q", r=4)
            for g in range(2):
                P.dma("sp", QT4[:], FM[R_QN + g * 256:R_QN + (g + 1) * 256, :].rearrange("(r d) t -> d r t", d=64), writes=[QT4])
                P.dma("sp", KsT[:], FM[R_KS + g * 64:R_KS + (g + 1) * 64, :], writes=[KsT])
                P.dma("sp", KwT[:], FM[R_KW + g * 64:R_KW + (g + 1) * 64, :], writes=[KwT])
                P.dma("sp", Vs[:], TMV[:, g, :].rearrange("(c p) e -> p c e", p=128), writes=[Vs])
                P.dma("sp", Vw[:], TMV[:, 2 + g, :].rearrange("(c p) e -> p c e", p=128), writes=[Vw])
                for i in range(NT):
                    isl = slice(i * 128, (i + 1) * 128)
                    cand = cands[i % 2]; addm = adds[i % 2]; siT = siTs[i % 2]; oaT = oaTs[i % 2]
                    P.dma("sp", cand[:], c_cand[i], writes=[cand])
                    P.dma("sp", addm[:], c_add[i], writes=[addm])
                    P.dma("sp", siT[:], FM[R_SA + g * 256:R_SA + (g + 1) * 256, isl].rearrange("(r d) q -> d r q", d=64), writes=[siT])
                    P.dma("sp", gl[64:65, :, :].rearrange("p x (r q) -> p x r q", r=4),
                          GT[12 * g:12 * g + 12, isl].rearrange("(r x) q -> x r q", x=3).unsqueeze(0), writes=[gl])
                    ncols = min(8 * i + 7, NCMP)
                    nchk = (ncols + 127) // 128
                    m0 = 1 if i == 0 else 0
                    lo = 8 * i - 1 + m0
                    for r in range(4):
                        b = nbS()
                        MM(b, [(b[:, 0:ncols], QT4[:, r, isl], KcT[g][:, 0:ncols], True, True)], [QT4, KcT[g]])
                        E = Es[r % 2]; Pn = Pns[r % 2]; rs = rss[r % 2]
                        ACT(E[:, 0:ncols], b[:, 0:ncols], AF.Exp, [b], [E], scale=0.125)
                        TT(E[:, lo:ncols], E[:, lo:ncols], mask8[:, m0:m0 + ncols - lo], ALU.mult, [E, mask8], [E])
                        P.op("dve", lambda e, E=E, rs=rs, nco=ncols: e.tensor_reduce(out=rs[:], in_=E[:, 0:nco], axis=AX.X, op=ALU.add), [E], [rs])
                        TS(rs[:], rs[:], 1e-30, None, ALU.max, None, [rs], [rs])
                        RECIP(rs[:], rs[:], [rs], [rs])
                        TS(Pn[:, 0:ncols], E[:, 0:ncols], rs[:, 0:1], None, ALU.mult, None, [E, rs], [Pn])
                        ws = [min(128, ncols - c * 128) for c in range(nchk)]
                        Btr = nbS()
                        btv = Btr[:].bitcast(BF16)
                        TR(Btr, [(btv[0:ws[c], c * 128:(c + 1) * 128], Pn[:, c * 128:c * 128 + ws[c]], identb[:]) for c in range(nchk)], [Pn, identb])
                        for c in range(nchk):
                            if c % 2 == 0:
                                ACT(PnT[r][0:ws[c], c, :], btv[0:ws[c], c * 128:(c + 1) * 128], AF.Copy, [Btr], [PnT[r]])
                            else:
                                CP(PnT[r][0:ws[c], c, :], btv[0:ws[c], c * 128:(c + 1) * 128], [Btr], [PnT[r]])
                        MM(Boc, [(Boc[0:64, r * 128:(r + 1) * 128], Vc[g][0:ws[c], c, :], PnT[r][0:ws[c], c, :], c == 0, c == nchk - 1)
                                 for c in range(nchk)], [Vc[g], PnT[r]])
                        MM(Bimp, [(Bimp[:, 0:128], PnT[r][0:ws[c], c, :], ovb[0:ws[c], c, :], r == 0 and c == 0, r == 3 and c == nchk - 1)
                                  for c in range(nchk)], [PnT[r], ovb])
                    TT(sc[:], Bimp[:, 0:128], cand[:], ALU.mult, [Bimp, cand], [sc])
                    TT(sc[:], sc[:], addm[:], ALU.add, [sc, addm], [sc])
                    P.op("dve", lambda e: e.max(out=m8a[:], in_=sc[:]), [sc], [m8a])
                    P.op("dve", lambda e: e.match_replace(out=scr[:], in_to_replace=m8a[:], in_values=sc[:], imm_value=-2.0), [sc, m8a], [scr])
                    P.op("dve", lambda e: e.max(out=m8b[:], in_=scr[:]), [scr], [m8b])
                    TS(thr[:], m8b[:, 7:8], 0.0, None, ALU.max, None, [m8b], [thr])
                    TS(M1[:], sc[:], thr[:, 0:1], 1.0, ALU.is_ge, ALU.subtract, [sc, thr], [M1])
                    Btr = nbS()
                    btv = Btr[:].bitcast(BF16)
                    TR(Btr, [(btv[:, 0:128], M1[:], identb[:])], [M1, identb])
                    ACT(MT4[:], btv[:, 0:128].unsqueeze(1).broadcast_to([128, 4, 128]), AF.Copy, [Btr], [MT4], scale=BIG)
                    pipe = Pipe(1)
                    c0 = max(0, i - 4)

                    def sel_step(cs):
                        bs = [nbS() for _ in cs]
                        pts = []
                        for _ in cs:
                            pts.append(PTs[pk[0] % len(PTs)]); pk[0] += 1

                        def s_fn():
                            specs = []
                            for c, b in zip(cs, bs):
                                specs.append((v4(b[:, :]), KsT[:, c * 128:(c + 1) * 128], QT4[:, :, isl], True, False))
                                specs.append((v4(b[:, :]), Eall[:, c, :], MT4[:], False, True))
                            MM(bs, specs, [KsT, QT4, Eall, MT4])

                        def mid_fn():
                            for c, b, pt in zip(cs, bs, pts):
                                ACT(pt[:], b[:, :], AF.Exp, [b], [pt], scale=0.125)
                                if c == i:
                                    TT(v4(pt[:]), v4(pt[:]), tri4, ALU.mult, [pt, trib], [pt])

                        def pv_fn():
                            MM(Bos, [(Bos[0:65, :], Vs[:, c, :], pt[:], c == 0, c == i) for c, pt in zip(cs, pts)], [Vs] + pts)
                        pipe.step(s_fn, mid_fn, pv_fn)

                    def win_step(cs):
                        bs = [nbS() for _ in cs]
                        pts = []
                        for _ in cs:
                            pts.append(PTs[pk[0] % len(PTs)]); pk[0] += 1

                        def s_fn():
                            MM(bs, [(v4(b[:, :]), KwT[:, c * 128:(c + 1) * 128], QT4[:, :, isl], True, True) for c, b in zip(cs, bs)], [KwT, QT4])

                        def mid_fn():
                            for c, b, pt in zip(cs, bs, pts):
                                ACT(pt[:], b[:, :], AF.Exp, [b], [pt], scale=0.125)
                                if c == i:
                                    TT(v4(pt[:]), v4(pt[:]), tri4, ALU.mult, [pt, trib], [pt])
                                if c == i - 4:
                                    TT(v4(pt[:]), v4(pt[:]), atri4, ALU.mult, [pt, atrib], [pt])

                        def pv_fn():
                            MM(Bow, [(Bow[0:65, :], Vw[:, c, :], pt[:], c == c0, c == i) for c, pt in zip(cs, pts)], [Vw] + pts)
                        pipe.step(s_fn, mid_fn, pv_fn)
                    wl = list(range(c0, i + 1))
                    for k_ in range(0, len(wl), 2):
                        win_step(wl[k_:k_ + 2])
                    sl = list(range(i + 1))
                    for k_ in range(0, len(sl), 2):
                        sel_step(sl[k_:k_ + 2])
                    pipe.flush()
                    CP(Fr[64:65, 0, :], gl[64:65, 0, :], [gl], [Fr])
                    RECIP(Fr[64:65, 1, :], Bos[64:65, :], [Bos], [Fr])
                    TT(Fr[64:65, 1, :], Fr[64:65, 1, :], gl[64:65, 1, :], ALU.mult, [Fr, gl], [Fr])
                    RECIP(Fr[64:65, 2, :], Bow[64:65, :], [Bow], [Fr])
                    TT(Fr[64:65, 2, :], Fr[64:65, 2, :], gl[64:65, 2, :], ALU.mult, [Fr, gl], [Fr])
                    for x_ in range(3):
                        b = nbS()
                        MM(b, [(b[0:64, :], onesf[64:65, 0:64], Fr[64:65, x_, :], True, True)], [onesf, Fr])
                        ACT(Rsb[:, x_, :], b[0:64, :], AF.Copy, [b], [Rsb])
                    def dump(x_, tb_):
                        if dbg:
                            P.dma("pool", DBG[x_, g * 256:(g + 1) * 256, isl].rearrange("(r d) q -> d r q", d=64), v4(tb_[:]), reads=[tb_])
                    TT(t1[:], Boc[0:64, :], Rsb[:, 0, :], ALU.mult, [Boc, Rsb], [t1])
                    dump(0, t1)
                    TT(t2[:], Bos[0:64, :], Rsb[:, 1, :], ALU.mult, [Bos, Rsb], [t2])
                    dump(1, t2)
                    TT(t1[:], t1[:], t2[:], ALU.add, [t1, t2], [t1])
                    TT(t2[:], Bow[0:64, :], Rsb[:, 2, :], ALU.mult, [Bow, Rsb], [t2])
                    dump(2, t2)
                    TT(t1[:], t1[:], t2[:], ALU.add, [t1, t2], [t1])
                    TT(oaT[:], v4(t1[:]), siT[:], ALU.mult, [t1, siT], [oaT])
                    P.dma("pool", OT[g * 256:(g + 1) * 256, isl].rearrange("(r d) q -> d r q", d=64), oaT[:], reads=[oaT])
            P.barrier()
            P.emit()

    def phase_A2(l):
        with ExitStack() as es:
            A = lambda n, s, d: Buf(es.enter_context(nc.sbuf_tensor("%s_L%d" % (n, l), s, d)))
            Xa = A("Xa", [128, NT, 8], F32); Xb = A("Xb", [128, NT, 8], F32)
            C = A("C", [128, NT, 8], F32); CR = A("CR", [128, NT, 8], F32)
            BAs = [A("BA%d" % i, [128, NT, NT], F32) for i in range(2)]
            FqTs = [A("FqT%d" % i, [64, T], BF16) for i in range(2)]
            FkTs = [A("FkT%d" % i, [64, T], BF16) for i in range(2)]
            Fvs = [A("Fv%d" % i, [128, NT, 65], BF16) for i in range(2)]
            PTs = [A("fPT%d" % i, [128, 512], BF16) for i in range(6)]
            rinv = A("rinv", [65, 512], F32); Rsb = A("fRsb", [64, 512], F32); t1 = A("ft1", [64, 512], F32)
            siTs = [A("fsiT%d" % i, [64, 512], BF16) for i in range(2)]
            obTs = [A("obT%d" % i, [64, 512], BF16) for i in range(2)]
            CP(Xa[:], LF[:], [LF], [Xa])
            src, dst = Xa, Xb
            s = 1
            while s < NT:
                TT(dst[:, s:, :], src[:, s:, :], src[:, :NT - s, :], ALU.add, [src], [dst])
                CP(dst[:, :s, :], src[:, :s, :], [src], [dst])
                src, dst = dst, src
                s *= 2
            TT(dst[:], src[:], LF[:], ALU.subtract, [src, LF], [dst])
            Xex = dst
            b = banks[7]
            f2 = lambda ap: ap.rearrange("p c h -> p (c h)")
            MM(b, [(b[:, 0:NT * 8], trif[:], f2(LF[:]), True, False), (b[:, 0:NT * 8], onesf[:], f2(Xex[:]), False, True)], [trif, LF, onesf, Xex])
            CP(f2(C[:]), b[:, 0:NT * 8], [b], [C])
            b = banks[6]
            MM(b, [(b[:, 0:NT * 8], self_[:], f2(C[:]), True, True)], [self_, C])
            CP(f2(CR[:]), b[:, 0:NT * 8], [b], [CR])
            sct = [0]; pk = [0]

            def nbS():
                bb = banks[sct[0] % 4]
                sct[0] += 1
                return bb
            qi = 0
            for hh in range(8):
                BA = BAs[hh % 2]; FqT = FqTs[hh % 2]; FkT = FkTs[hh % 2]; Fv = Fvs[hh % 2]
                TT(BA[:], CR[:, :, hh].unsqueeze(2).broadcast_to([128, NT, NT]), C[:, :, hh].unsqueeze(1).broadcast_to([128, NT, NT]),
                   ALU.subtract, [CR, C], [BA])
                P.dma("sp", FqT[:], FM[R_FQ + hh * 64:R_FQ + (hh + 1) * 64, :], writes=[FqT])
                P.dma("sp", FkT[:], FM[R_FK + hh * 64:R_FK + (hh + 1) * 64, :], writes=[FkT])
                P.dma("sp", Fv[:], TMV[:, 4 + hh, :].rearrange("(c p) e -> p c e", p=128), writes=[Fv])
                for qg in range(NT // 4):
                    qsl = slice(qg * 512, (qg + 1) * 512)
                    Bo = banks[4 + qi % 2]; siT = siTs[qi % 2]; obT = obTs[qi % 2]; qi += 1
                    P.dma("sp", siT[:], FM[R_SB + hh * 64:R_SB + (hh + 1) * 64, qsl], writes=[siT])
                    nk = 4 * qg + 4
                    pipe = Pipe(1)

                    def fox_step(kcs):
                        bs = [nbS() for _ in kcs]
                        pts = []
                        for _ in kcs:
                            pts.append(PTs[pk[0] % len(PTs)]); pk[0] += 1

                        def s_fn():
                            MM(bs, [(b[:, :], FkT[:, kc * 128:(kc + 1) * 128], FqT[:, qsl], True, True) for kc, b in zip(kcs, bs)], [FkT, FqT])

                        def mid_fn():
                            for kc, b, pt in zip(kcs, bs, pts):
                                j0 = max(0, kc - 4 * qg)
                                for s_ in range(j0, 4):
                                    qb = 4 * qg + s_
                                    ssl = slice(s_ * 128, (s_ + 1) * 128)
                                    ACT(pt[:, ssl], b[:, ssl], AF.Exp, [b, BA], [pt], scale=0.125, bias=BA[:, qb, kc:kc + 1])
                                if kc >= 4 * qg:
                                    ssl = slice(j0 * 128, (j0 + 1) * 128)
                                    TT(pt[:, ssl], pt[:, ssl], trib[:], ALU.mult, [pt, trib], [pt])

                        def pv_fn():
                            specs = []
                            for kc, pt in zip(kcs, pts):
                                j0 = max(0, kc - 4 * qg)
                                specs.append((Bo[0:65, j0 * 128:512], Fv[:, kc, :], pt[:, j0 * 128:512], kc == 0, kc == nk - 1))
                            MM(Bo, specs, [Fv] + pts)
                        pipe.step(s_fn, mid_fn, pv_fn)
                    for kc in range(0, nk, 2):
                        fox_step([kc, kc + 1])
                    pipe.flush()
                    RECIP(rinv[64:65, :], Bo[64:65, :], [Bo], [rinv])
                    b = banks[6 + qi % 2]
                    MM(b, [(b[0:64, :], onesf[64:65, 0:64], rinv[64:65, :], True, True)], [onesf, rinv])
                    ACT(Rsb[:], b[0:64, :], AF.Copy, [b], [Rsb])
                    TT(t1[:], Bo[0:64, :], Rsb[:], ALU.mult, [Bo, Rsb], [t1])
                    TT(obT[:], t1[:], siT[:], ALU.mult, [t1, siT], [obT])
                    P.dma("pool", OT[512 + hh * 64:512 + (hh + 1) * 64, qsl], obT[:], reads=[obT])
            P.barrier()
            P.emit()

    def phase_A3(l):
        with ExitStack() as es:
            A = lambda n, s, d: Buf(es.enter_context(nc.sbuf_tensor("%s_L%d" % (n, l), s, d)))
            stg = [A("mstg%d" % i, [128, 1024], F32) for i in range(2)]
            cnt = [0]
            Wm = A("Wm", [128, 8, 1024], BF16)
            for kc in range(8):
                load_cast(stg, cnt, Wm, Wm[:, kc, :], w_mem_kv[l, kc * 128:(kc + 1) * 128, :], 1024)
            gm = A("gm", [128, 1024], F32); gk1 = A("gk1", [128, 128], F32); Gk4 = A("Gk4", [128, 512], F32)
            P.dma("sp", gm[:], mem_norm_g[l].partition_broadcast(128), writes=[gm])
            P.dma("sp", gk1[:], mem_k_norm[l].partition_broadcast(128), writes=[gk1])
            CP(Gk4[:].rearrange("p (h d) -> p h d", h=4), gk1[:].unsqueeze(1).broadcast_to([128, 4, 128]), [gk1], [Gk4], "pool")
            MkT = A("MkT", [128, 4, 256], BF16); Mv = A("Mv", [128, 2, 4, 128], BF16)
            xt = A("mxt", [128, 1024], F32); junk = A("mjunk", [128, 1024], BF16); ss = A("mss", [128, 1], F32)
            h = A("mh", [128, 1024], BF16); hT = A("mhT", [128, 8, 128], BF16)
            xs = A("mxs", [128, 512], F32); sq = A("msq", [128, 512], F32); ssq = A("mssq", [128, 4], F32); kn = A("mkn", [128, 512], BF16)
            v3 = lambda ap: ap.rearrange("p (h d) -> p h d", h=4)
            for mt in range(2):
                P.dma("sp", xt[:], mem_in[mt * 128:(mt + 1) * 128, :], writes=[xt])
                rmsnorm_rows(A, xt, gm, h, junk, ss)
                transpose8(h, hT)
                b = nb()
                MM(b, [(b[:, :], hT[:, kc, :], Wm[:, kc, 0:512], kc == 0, kc == 7) for kc in range(8)], [hT, Wm])
                ACT(xs[:], b[:, :], AF.Copy, [b], [xs])
                ACT(sq[:], b[:, :], AF.Square, [b], [sq])
                P.op("dve", lambda e: e.tensor_reduce(out=ssq[:], in_=v3(sq[:]), axis=AX.X, op=ALU.add), [sq], [ssq])
                TS(ssq[:], ssq[:], 1.0 / 128, EPS, ALU.mult, ALU.add, [ssq], [ssq])
                ACT(ssq[:], ssq[:], AF.Sqrt, [ssq], [ssq])
                RECIP(ssq[:], ssq[:], [ssq], [ssq])
                TT(v3(xs[:]), v3(xs[:]), ssq[:].unsqueeze(2).broadcast_to([128, 4, 128]), ALU.mult, [xs, ssq], [xs])
                TT(kn[:], xs[:], Gk4[:], ALU.mult, [xs, Gk4], [kn])
                b = nb()
                bv = b[:].bitcast(BF16).rearrange("p (c t) -> p c t", c=8)
                TR(b, [(bv[:, j, :], kn[:, j * 128:(j + 1) * 128], identb[:]) for j in range(4)], [kn, identb])
                CP(MkT[:, :, mt * 128:(mt + 1) * 128], bv[:, 0:4, :], [b], [MkT])
                b = nb()
                MM(b, [(b[:, :], hT[:, kc, :], Wm[:, kc, 512:1024], kc == 0, kc == 7) for kc in range(8)], [hT, Wm])
                CP(Mv[:, mt, :, :], v3(b[:, :]), [b], [Mv])
            MqTs = [A("MqT%d" % i, [128, T], BF16) for i in range(2)]
            PTs = [A("mPT%d" % i, [128, 512], BF16) for i in range(3)]
            rv = A("mrv", [128, 512], F32); t1 = A("mt1", [128, 512], F32)
            siTs = [A("msiT%d" % i, [128, 512], BF16) for i in range(2)]
            omTs = [A("omT%d" % i, [128, 512], BF16) for i in range(2)]
            pk = [0]; qi = 0; sct = [0]
            for hm in range(4):
                MqT = MqTs[hm % 2]
                P.dma("sp", MqT[:], FM[R_MQ + hm * 128:R_MQ + (hm + 1) * 128, :], writes=[MqT])
                for qg in range(T // 512):
                    qsl = slice(qg * 512, (qg + 1) * 512)
                    Bo = banks[4 + qi % 2]; Brs = banks[6 + qi % 2]; siT = siTs[qi % 2]; omT = omTs[qi % 2]; qi += 1
                    P.dma("sp", siT[:], FM[R_SM + hm * 128:R_SM + (hm + 1) * 128, qsl], writes=[siT])
                    for mc in range(2):
                        b = banks[sct[0] % 4]; sct[0] += 1
                        MM(b, [(b[:, :], MkT[:, hm, mc * 128:(mc + 1) * 128], MqT[:, qsl], True, True)], [MkT, MqT])
                        pt = PTs[pk[0] % 3]; pk[0] += 1
                        ACT(pt[:], b[:, :], AF.Exp, [b], [pt], scale=float(128 ** -0.5))
                        MM(Bo, [(Bo[:, :], Mv[:, mc, hm, :], pt[:], mc == 0, mc == 1)], [Mv, pt])
                        MM(Brs, [(Brs[:, :], onesb[:], pt[:], mc == 0, mc == 1)], [onesb, pt])
                    RECIP(rv[:], Brs[:, :], [Brs], [rv])
                    TT(t1[:], Bo[:, :], rv[:], ALU.mult, [Bo, rv], [t1])
                    TT(omT[:], t1[:], siT[:], ALU.mult, [t1, siT], [omT])
                    P.dma("pool", OT[1024 + hm * 128:1024 + (hm + 1) * 128, qsl], omT[:], reads=[omT])
            P.barrier()
            P.emit()

    def phase_E(l, x_src):
        with ExitStack() as es:
            A = lambda n, s, d: Buf(es.enter_context(nc.sbuf_tensor("%s_L%d" % (n, l), s, d)))
            stg = [A("estg%d" % i, [128, 2048], F32) for i in range(2)]
            cnt = [0]
            Wg = A("Wg", [128, 8, 3072], BF16); Wb3 = A("Wb3", [128, 12, 1024], BF16); Wo = A("Wo", [128, 8, 1024], BF16)
            for kc in range(8):
                ksl = slice(kc * 128, (kc + 1) * 128)
                load_cast(stg, cnt, Wg, Wg[:, kc, 0:2048], w_in[l, ksl, 4896:6944], 2048)
                load_cast(stg, cnt, Wg, Wg[:, kc, 2048:3072], w_in[l, ksl, 6944:7968], 1024)
                load_cast(stg, cnt, Wo, Wo[:, kc, :], w_out[l, ksl, :], 1024)
            for xb in range(3):
                for kc in range(4):
                    load_cast(stg, cnt, Wb3, Wb3[:, 4 * xb + kc, :], w_br[xb][l, kc * 128:(kc + 1) * 128, :], 1024)
            gn = A("egn", [128, 1024], F32)
            P.dma("sp", gn[:], norm_g[l].partition_broadcast(128), writes=[gn])
            xts = [A("ext%d" % i, [128, 1024], F32) for i in range(2)]
            oTs = [A("eoT%d" % i, [128, 12, 128], BF16) for i in range(2)]
            junk = A("ejunk", [128, 1024], BF16); ss = A("ess", [128, 1], F32)
            h = A("eh", [128, 1024], BF16); hT = A("ehT", [128, 8, 128], BF16)
            sg = A("esg", [128, 3072], F32); u = A("eu", [128, 1024], F32); tmp = A("etmp", [128, 512], F32)
            ub = A("eub", [128, 1024], BF16); uT = A("euT", [128, 8, 128], BF16)
            xos = [A("exo%d" % i, [128, 1024], F32) for i in range(2)]
            for tt in range(NT):
                tsl = slice(tt * 128, (tt + 1) * 128)
                xt = xts[tt % 2]; oT = oTs[tt % 2]; xo = xos[tt % 2]
                P.dma("sp", xt[:], x_src[tsl, :], writes=[xt])
                P.dma("sp", oT[:], OT[:, tsl].rearrange("(c p) t -> p c t", p=128), writes=[oT])
                rmsnorm_rows(A, xt, gn, h, junk, ss)
                transpose8(h, hT)
                for cc in range(6):
                    b = nb()
                    MM(b, [(b[:, :], hT[:, kc, :], Wg[:, kc, cc * 512:(cc + 1) * 512], kc == 0, kc == 7) for kc in range(8)], [hT, Wg])
                    ACT(sg[:, cc * 512:(cc + 1) * 512], b[:, :], AF.Sigmoid, [b], [sg])
                for xb in range(3):
                    for cc in range(2):
                        csl = slice(cc * 512, (cc + 1) * 512)
                        b = nb()
                        MM(b, [(b[:, :], oT[:, 4 * xb + kc, :], Wb3[:, 4 * xb + kc, csl], kc == 0, kc == 3) for kc in range(4)], [oT, Wb3])
                        gsl = slice(xb * 1024 + cc * 512, xb * 1024 + (cc + 1) * 512)
                        if xb == 0:
                            TT(u[:, csl], b[:, :], sg[:, gsl], ALU.mult, [b, sg], [u])
                        else:
                            TT(tmp[:], b[:, :], sg[:, gsl], ALU.mult, [b, sg], [tmp])
                            TT(u[:, csl], u[:, csl], tmp[:], ALU.add, [u, tmp], [u])
                CP(ub[:], u[:], [u], [ub], "pool")
                transpose8(ub, uT)
                for cc in range(2):
                    csl = slice(cc * 512, (cc + 1) * 512)
                    b = nb()
                    MM(b, [(b[:, :], uT[:, kc, :], Wo[:, kc, csl], kc == 0, kc == 7) for kc in range(8)], [uT, Wo])
                    TT(xo[:, csl], b[:, :], xt[:, csl], ALU.add, [b, xt], [xo])
                P.dma("pool", out[tsl, :], xo[:], reads=[xo])
            P.barrier()
            P.emit()

    for l in range(L):
        xs_ = x_in if l == 0 else out
        if "P" in phases:
            phase_P(l, xs_)
        if "C" in phases:
            phase_C(l)
        if "1" in phases:
            phase_A1(l)
        if "2" in phases:
            phase_A2(l)
        if "3" in phases:
            phase_A3(l)
        if "E" in phases:
            phase_E(l, xs_)
    gs.close()
    return nc


def make_consts(T):
    NT = T // 128
    NCMP = T // 16 - 1
    NCH = (NCMP + 127) // 128
    f = np.float32
    half = 32
    inv = (10000.0 ** (-np.arange(half, dtype=np.float32) / half)).astype(np.float32)
    ang = np.arange(T, dtype=np.float32)[:, None] * inv[None, :]
    cos = np.cos(ang).astype(f); sin = np.sin(ang).astype(f)
    k = np.arange(128)
    tri = (k[:, None] <= k[None, :]).astype(f)
    atri = (k[:, None] > k[None, :]).astype(f)
    sel = np.zeros((128, 128), f); sel[63, :] = 1.0
    eall = np.zeros((128, NT, 128), f)
    for c in range(NT):
        eall[2 * c, c, :64] = 1.0
        eall[2 * c + 1, c, 64:] = 1.0
    cand = np.zeros((NT, 128, 128), f); addm = np.zeros((NT, 128, 128), f)
    j = np.arange(128)[None, :]
    for i in range(NT):
        cur = (2 * i + (k >= 64).astype(np.int64))[:, None]
        forced = (j == 0) | (j == cur) | (j == cur - 1)
        cnd = (j >= 1) & (j <= cur - 2)
        cand[i] = cnd.astype(f)
        addm[i] = np.where(forced, 8.0, np.where(cnd, 0.0, -1.0)).astype(f)
    n_slc = T // 64
    cs = np.arange(NCMP)[:, None] * 16
    ss = np.arange(128)[None, :] * 64
    ov = np.clip(np.minimum(cs + 32, ss + 64) - np.maximum(cs, ss), 0, None).astype(f) / 32.0
    ov[:, n_slc:] = 0.0
    ovp = np.zeros((NCH * 128, 128), f); ovp[:NCMP] = ov
    m8 = (k[:, None] >= 16 * np.arange(8)[None, :] + 15).astype(f)
    return dict(c_ident=np.eye(128, dtype=f), c_ropec=np.concatenate([cos, cos], 1), c_ropes=np.concatenate([-sin, sin], 1),
                c_tri=tri, c_atri=atri, c_sel=sel, c_eall=eall.reshape(128, NT * 128), c_cand=cand, c_add=addm, c_ov=ovp, c_mask8=m8)


WNAMES = ["norm_g", "mem_norm_g", "w_in", "nsa_q_norm", "nsa_k_norm", "cmp_pe_k", "cmp_w1_k", "cmp_w2_k", "cmp_pe_v", "cmp_w1_v",
          "cmp_w2_v", "fox_q_norm", "fox_k_norm", "fox_f_bias", "mem_q_norm", "mem_k_norm", "w_mem_kv", "w_branch_a", "w_branch_b",
          "w_branch_m", "w_out"]


def run(inputs, T, L, dbg=False, n_cores=8, phases="PC123E", trace=False):
    nc = bass.Bass("TRN2", target_bir_lowering=False)
    build(nc, T, L, dbg, phases)
    consts = make_consts(T)
    B = inputs["x"].shape[0]
    in_maps = []
    for c in range(n_cores):
        b = c % B
        m = {"x": np.ascontiguousarray(inputs["x"][b], dtype=np.float32), "mem": np.ascontiguousarray(inputs["mem"][b], dtype=np.float32)}
        for n in WNAMES:
            m[n] = np.ascontiguousarray(inputs[n][:L], dtype=np.float32)
        m.update(consts)
        in_maps.append(m)
    res = run_bass_kernel_spmd(nc, in_maps, core_ids=list(range(n_cores)), **({"trace": True} if trace else {}))
    return res


def kernel(**inputs):
    inputs = {k: np.asarray(v) for k, v in inputs.items()}
    res = run(inputs, 8192, 4)
    B = inputs["x"].shape[0]
    return np.stack([np.asarray(res.results[b]["out"], dtype=np.float32) for b in range(B)], axis=0)
```
